# Optimizing a Trainium2 kernel written in Bass

```python
import math
import jax, jax.numpy as jnp
from jax import lax
import numpy as np

D_MODEL = 2048
BATCH = 8
SEQ = 2048
DEPTH = 1

GRID_W = 64
Q_BLOCK = 128
A_HEADS = 8
A_HEAD_DIM = 64
A_V_DIM = 2 * A_HEAD_DIM
A_WIDTH = A_HEADS * A_V_DIM
B_HEADS = 8
B_KV_HEADS = 2
B_GROUP = B_HEADS // B_KV_HEADS
B_HEAD_DIM = 128
B_WIDTH = B_HEADS * B_HEAD_DIM
ROPE_SECTION = B_HEAD_DIM // 2
ROPE_THETA = 10000.0
N_BRANCHES = 2
IN_SIZES = (
    A_HEADS * 2 * A_HEAD_DIM,
    A_HEADS * 2 * A_HEAD_DIM,
    A_WIDTH,
    A_WIDTH,
    B_HEADS * B_HEAD_DIM,
    B_KV_HEADS * B_HEAD_DIM,
    B_KV_HEADS * B_HEAD_DIM,
    B_WIDTH,
    N_BRANCHES * D_MODEL,
)
IN_COLS = sum(IN_SIZES)
NORM_EPS = 1e-6

kernel_name = "hybrid_diffattn_gqa_axialrope_gated_merge"


def rms_norm(x, g):
    xf = x.astype(jnp.float32)
    y = xf * lax.rsqrt(jnp.mean(xf * xf, axis=-1, keepdims=True) + NORM_EPS)
    return (y * g.astype(jnp.float32)).astype(x.dtype)


def to_blocks(t, axis):
    shape = t.shape
    nb = shape[axis] // Q_BLOCK
    t = t.reshape(shape[:axis] + (nb, Q_BLOCK) + shape[axis + 1:])
    return jnp.moveaxis(t, axis, 0)


def from_blocks(t, axis):
    t = jnp.moveaxis(t, 0, axis)
    shape = t.shape
    return t.reshape(shape[:axis] + (shape[axis] * shape[axis + 1],) + shape[axis + 2:])


def diff_attention(q, k, v, lam, slopes):
    S = q.shape[3]
    pos = jnp.arange(S, dtype=jnp.float32)
    scale = A_HEAD_DIM ** -0.5

    def one_block(args):
        qi, pi = args
        s = jnp.einsum('bhmqd,bhmkd->bhmqk', qi, k).astype(jnp.float32) * scale
        s = s - slopes[:, None, None, None] * jnp.abs(pi[:, None] - pos[None, :])
        p = jax.nn.softmax(s, axis=-1)
        w = p[:, :, 0] - lam * p[:, :, 1]
        return jnp.einsum('bhqk,bhkd->bhqd', w.astype(v.dtype), v)

    o = lax.map(one_block, (to_blocks(q, 3), to_blocks(pos, 0)))
    return from_blocks(o, 2)


def gqa_attention(q, k, v):
    scale = B_HEAD_DIM ** -0.5

    def one_block(qi):
        s = jnp.einsum('bkgqd,bksd->bkgqs', qi, k).astype(jnp.float32) * scale
        p = jax.nn.softmax(s, axis=-1)
        return jnp.einsum('bkgqs,bksd->bkgqd', p.astype(v.dtype), v)

    o = lax.map(one_block, to_blocks(q, 3))
    return from_blocks(o, 3)


def rotate_section(xs, ang):
    x1, x2 = jnp.split(xs, 2, axis=-1)
    c = jnp.cos(ang).astype(xs.dtype)
    s = jnp.sin(ang).astype(xs.dtype)
    return jnp.concatenate([x1 * c - x2 * s, x2 * c + x1 * s], axis=-1)


def axial_rope(t, row, col):
    inv = ROPE_THETA ** (-jnp.arange(0, ROPE_SECTION, 2, dtype=jnp.float32) / ROPE_SECTION)
    ang_r = row[:, None] * inv[None, :]
    ang_c = col[:, None] * inv[None, :]
    return jnp.concatenate([rotate_section(t[..., :ROPE_SECTION], ang_r),
                            rotate_section(t[..., ROPE_SECTION:], ang_c)], axis=-1)


def setup_inputs(seed: int = 0) -> dict:
    key = jax.random.key(seed)
    ks = jax.random.split(key, 16)
    f32 = jnp.float32

    def gain(k, n):
        return (jnp.ones((DEPTH, n), f32) + 0.01 * jax.random.normal(k, (DEPTH, n), f32))

    return {
        "x": jax.random.normal(ks[0], (BATCH, SEQ, D_MODEL), f32),
        "norm_g": gain(ks[1], D_MODEL),
        "w_in": jax.random.normal(ks[2], (DEPTH, D_MODEL, IN_COLS), f32) * D_MODEL ** -0.5,
        "a_lambda_q1": 0.1 * jax.random.normal(ks[3], (DEPTH, A_HEAD_DIM), f32),
        "a_lambda_k1": 0.1 * jax.random.normal(ks[4], (DEPTH, A_HEAD_DIM), f32),
        "a_lambda_q2": 0.1 * jax.random.normal(ks[5], (DEPTH, A_HEAD_DIM), f32),
        "a_lambda_k2": 0.1 * jax.random.normal(ks[6], (DEPTH, A_HEAD_DIM), f32),
        "a_subln_g": gain(ks[7], A_V_DIM),
        "b_qnorm_g": gain(ks[8], B_HEAD_DIM),
        "b_knorm_g": gain(ks[9], B_HEAD_DIM),
        "w_proj_a": jax.random.normal(ks[10], (DEPTH, A_WIDTH, D_MODEL), f32) * A_WIDTH ** -0.5,
        "w_proj_b": jax.random.normal(ks[11], (DEPTH, B_WIDTH, D_MODEL), f32) * B_WIDTH ** -0.5,
        "w_out": jax.random.normal(ks[12], (DEPTH, D_MODEL, D_MODEL), f32) * D_MODEL ** -0.5,
        "final_g": jnp.ones((D_MODEL,), f32) + 0.01 * jax.random.normal(ks[13], (D_MODEL,), f32),
    }


def reference(x, norm_g, w_in, a_lambda_q1, a_lambda_k1, a_lambda_q2, a_lambda_k2,
              a_subln_g, b_qnorm_g, b_knorm_g, w_proj_a, w_proj_b, w_out, final_g):
    B, S, _ = x.shape
    rows = S // GRID_W
    row = jnp.repeat(jnp.arange(rows), GRID_W).astype(jnp.float32)
    col = jnp.tile(jnp.arange(GRID_W), rows).astype(jnp.float32)
    slopes = 2.0 ** (-8.0 * jnp.arange(1, A_HEADS + 1, dtype=jnp.float32) / A_HEADS)
    offsets = [int(o) for o in np.cumsum(IN_SIZES)[:-1]]

    h = x
    for l in range(DEPTH):
        u = rms_norm(h, norm_g[l])
        z = u @ w_in[l]
        aq, ak, av, ag, bq, bk, bv, bg, gm = jnp.split(z, offsets, axis=-1)

        aq = aq.reshape(B, S, A_HEADS, 2, A_HEAD_DIM).transpose(0, 2, 3, 1, 4)
        ak = ak.reshape(B, S, A_HEADS, 2, A_HEAD_DIM).transpose(0, 2, 3, 1, 4)
        av = av.reshape(B, S, A_HEADS, A_V_DIM).transpose(0, 2, 1, 3)
        lam_init = 0.8 - 0.6 * math.exp(-0.3 * l)
        lam = (jnp.exp(jnp.sum(a_lambda_q1[l].astype(jnp.float32) * a_lambda_k1[l].astype(jnp.float32)))
               - jnp.exp(jnp.sum(a_lambda_q2[l].astype(jnp.float32) * a_lambda_k2[l].astype(jnp.float32)))
               + lam_init)
        oa = diff_attention(aq, ak, av, lam, slopes)
        oa = rms_norm(oa, a_subln_g[l]) * (1.0 - lam_init)
        oa = oa.transpose(0, 2, 1, 3).reshape(B, S, A_WIDTH) * jax.nn.silu(ag)

        bq = rms_norm(bq.reshape(B, S, B_HEADS, B_HEAD_DIM), b_qnorm_g[l])
        bk = rms_norm(bk.reshape(B, S, B_KV_HEADS, B_HEAD_DIM), b_knorm_g[l])
        bq = bq.reshape(B, S, B_KV_HEADS, B_GROUP, B_HEAD_DIM).transpose(0, 2, 3, 1, 4)
        bk = bk.transpose(0, 2, 1, 3)
        bq = axial_rope(bq, row, col)
        bk = axial_rope(bk, row, col)
        bv = bv.reshape(B, S, B_KV_HEADS, B_HEAD_DIM).transpose(0, 2, 1, 3)
        ob = gqa_attention(bq, bk, bv)
        ob = ob.transpose(0, 3, 1, 2, 4).reshape(B, S, B_WIDTH) * jax.nn.silu(bg)

        ya = oa @ w_proj_a[l]
        yb = ob @ w_proj_b[l]
        ga, gb = jnp.split(jax.nn.sigmoid(gm), N_BRANCHES, axis=-1)
        h = h + (ga * ya + gb * yb) @ w_out[l]

    return rms_norm(h, final_g)
```

```python
import math
from contextlib import ExitStack

import numpy as np
import concourse.bass as bass
import concourse.mybir as mybir
from concourse.bass_utils import run_bass_kernel_spmd

F32 = mybir.dt.float32
BF16 = mybir.dt.bfloat16
U8 = mybir.dt.uint8
AF = mybir.ActivationFunctionType
ALU = mybir.AluOpType
AX = mybir.AxisListType

D = 2048
KC = 16
INC = 10752
OFF_AQ, OFF_AK, OFF_AV, OFF_AG = 0, 1024, 2048, 3072
OFF_BQ, OFF_BK, OFF_BV, OFF_BG, OFF_GM = 4096, 5120, 5376, 5632, 6656
EPS = 1e-6
GRID_W = 64
LAM_INIT = 0.8 - 0.6 * math.exp(0.0)
NW = 5


class Buf:
    __slots__ = ("name", "w", "wd", "r", "rd", "pr", "prd", "scratch")

    def __init__(self, name, scratch=False):
        self.name = name
        self.w = {}
        self.wd = []
        self.r = {}
        self.rd = []
        self.pr = {}
        self.prd = []
        self.scratch = scratch


class Ins:
    __slots__ = ("eng", "fn", "dma", "deps", "sig", "sem", "val", "ninc", "key")

    def __init__(self, eng, fn, dma, ninc, key):
        self.eng = eng
        self.fn = fn
        self.dma = dma
        self.deps = set()
        self.sig = False
        self.sem = None
        self.val = 0
        self.ninc = ninc
        self.key = key


class Prog:
    ENGS = ("pe", "act", "dve", "pool", "sp")

    def __init__(self):
        self.streams = {e: [] for e in self.ENGS}
        self.fence_deps = set()
        self.dmas_since_fence = []

    def op(self, eng, fn, reads=(), writes=(), pwrites=(), dma_key=None, ninc=1):
        dma = dma_key is not None
        ins = Ins(eng, fn, dma, ninc, dma_key)
        deps = ins.deps
        scratch = False
        for b in reads:
            deps.update(b.w.values())
            deps.update(b.wd)
            scratch |= b.scratch
        for b in writes:
            deps.update(b.w.values())
            deps.update(b.wd)
            deps.update(b.r.values())
            deps.update(b.rd)
            deps.update(b.pr.values())
            deps.update(b.prd)
            scratch |= b.scratch
        for b in pwrites:
            deps.update(b.r.values())
            deps.update(b.rd)
            deps.update(b.pr.values())
            deps.update(b.prd)
            scratch |= b.scratch
        if scratch:
            deps.update(self.fence_deps)
        deps.discard(ins)
        for b in reads:
            if dma:
                b.rd.append(ins)
            else:
                b.r[eng] = ins
        for b in writes:
            b.pr, b.prd = b.r, b.rd
            b.w, b.wd, b.r, b.rd = {}, [], {}, []
            if dma:
                b.wd.append(ins)
            else:
                b.w[eng] = ins
        for b in pwrites:
            if b.r or b.rd:
                b.pr, b.prd = b.r, b.rd
                b.w, b.wd, b.r, b.rd = {}, [], {}, []
            if dma:
                b.wd.append(ins)
            else:
                b.w[eng] = ins
        self.streams[eng].append(ins)
        if dma:
            self.dmas_since_fence.append(ins)
        return ins

    def fence(self):
        f = set()
        for e in self.ENGS:
            if self.streams[e]:
                f.add(self.streams[e][-1])
        f.update(self.dmas_since_fence)
        self.fence_deps = set(i for i in f)
        self.dmas_since_fence = []

    def finalize(self, nc, stack):
        for e in self.ENGS:
            for ins in self.streams[e]:
                for d in ins.deps:
                    if (not d.dma) and (not ins.dma) and d.eng == "pe" and ins.eng == "pe":
                        continue
                    d.sig = True
        self.esem = {}
        for e in ("pe", "act", "dve"):
            self.esem[e] = stack.enter_context(nc.semaphore("s_" + e))
        self.dsem = {}
        dcount = {}
        for e in self.ENGS:
            cnt = 0
            for ins in self.streams[e]:
                if ins.dma:
                    if ins.key not in self.dsem:
                        self.dsem[ins.key] = stack.enter_context(nc.semaphore("d_" + ins.key))
                        dcount[ins.key] = 0
                    dcount[ins.key] += 16 * ins.ninc
                    ins.sem = self.dsem[ins.key]
                    ins.val = dcount[ins.key]
                elif ins.sig:
                    assert e in self.esem, e
                    cnt += 1
                    ins.sem = self.esem[e]
                    ins.val = cnt

    def emit(self, eng, engobj):
        waited = {}
        for ins in self.streams[eng]:
            need = {}
            for d in ins.deps:
                if (not d.dma) and (not ins.dma) and d.eng == "pe" and eng == "pe":
                    continue
                k = id(d.sem)
                if waited.get(k, 0) >= d.val:
                    continue
                if k not in need or need[k][1] < d.val:
                    need[k] = (d.sem, d.val)
            for k, (sem, val) in need.items():
                engobj.wait_ge(sem, val)
                waited[k] = val
            r = ins.fn(engobj)
            if ins.dma:
                assert len(r) == ins.ninc, (len(r), ins.ninc)
                for x in r:
                    x.then_inc(ins.sem, 16)
            elif ins.sig:
                r.then_inc(ins.sem, 1)


def build_nc(S=2048, dbg=None, phases=("p0", "A", "B", "E")):
    assert S % 512 == 0
    NB = S // 128
    BS = 512
    TB = S // BS
    HS = S // 2
    EBS = min(512, HS)
    ETB = HS // EBS
    ROWS = S // GRID_W

    nc = bass.Bass("TRN2", target_bir_lowering=False)
    try:
        nc.allow_low_precision("bf16 matmul operands with fp32 PSUM accumulation")
    except Exception:
        pass

    def din(name, shape):
        return nc.dram_tensor(name, list(shape), F32, kind="ExternalInput").ap()

    x_d = din("x", [S, D])
    win_d = din("w_in", [D, INC])
    wpa_d = din("w_proj_a", [1024, D])
    wpb_d = din("w_proj_b", [1024, D])
    wout_d = din("w_out", [D, D])
    pcol_d = din("pcol", [128, 19])
    lamv_d = din("lamv", [128, 256])
    fg_d = din("fgb", [128, D])
    cid_d = din("cident", [128, 512])
    NDELTA = (2 * S - 128 - 512) // 128 + 1
    qaug_d = din("qaug", [128, S])
    kaug_d = din("kaug", [128, S])
    crope_d = din("crope", [128, 256])
    out_d = nc.dram_tensor("out", [S, D], F32, kind="ExternalOutput").ap()

    dbg_outs = []

    P = Prog()
    stack = ExitStack()

    def sb(name, shape, dt):
        return nc.alloc_sbuf_tensor(name, list(shape), dt).ap()

    uT = sb("uT", [128, KC * S], BF16)
    uT3 = uT.rearrange("p (k s) -> p k s", k=KC)
    oag = sb("oag", [128, 8 * S], BF16)
    oag3 = oag.rearrange("p (k s) -> p k s", k=8)
    obg = sb("obg", [128, 8 * S], BF16)
    obg3 = obg.rearrange("p (k s) -> p k s", k=8)
    wsl = sb("wsl", [128, NW * KC * 128], BF16)
    wsl4 = wsl.rearrange("p (n k c) -> p n k c", n=NW, k=KC)
    CID = sb("cid", [128, 512], BF16)
    ident = CID[:, 0:128]
    ones = CID[:, 128:256]
    swapm = CID[:, 256:384]
    diagm = CID[:, 384:512]
    PCOL = sb("pcol_s", [128, 19], F32)
    LAMV = sb("lamv_s", [128, 256], F32)
    SMALL = sb("small", [128, 64], F32)
    SSQ = sb("ssq", [128, 4 * NB], F32)
    SCRB = 54784
    SCR = sb("scr", [128, SCRB], U8)
    if S != 2048:
        WOUT = sb("wout_t", [128, KC * D], BF16)

    uTb = [Buf(f"uT{n}") for n in range(NB)]
    oagb = [Buf(f"oag{h}") for h in range(8)]
    obgb = [Buf(f"obg{h}") for h in range(8)]
    wslb = [Buf(f"wsl{i}") for i in range(NW)]
    cidb = Buf("cid")
    pcolb = Buf("pcol")
    lamvb = Buf("lamv")
    smallb = Buf("small")
    ssqb = Buf("ssq")

    PST = [nc.alloc_psum_tensor("psA", [128, 2048], F32).ap(), nc.alloc_psum_tensor("psB", [128, 2048], F32).ap()]
    PS = [PST[j // 4][:, (j % 4) * 512:(j % 4 + 1) * 512] for j in range(8)]
    PSb = [Buf(f"ps{j}") for j in range(8)]

    def psbf(j, n):
        return PS[j][:, 0:n // 2].bitcast(BF16)

    class Carver:
        def __init__(self):
            self.off = 0

        def take(self, nbytes_per_part, dt, name):
            nb_ = (nbytes_per_part + 31) // 32 * 32
            assert self.off + nb_ <= SCRB, (name, self.off, nb_, SCRB)
            ap = SCR[:, self.off:self.off + nbytes_per_part]
            self.off += nb_
            if dt is not U8:
                ap = ap.bitcast(dt)
            return ap, Buf(name, scratch=True)

    wctr = [0]
    head_carry = []

    def load_w(src_ap, kcn, label):
        i = wctr[0] % NW
        wctr[0] += 1
        dst = wsl4[:, i, 0:kcn, :]
        src = src_ap.rearrange("(k p) c -> p k c", p=128)
        P.op("pool", lambda e: [e.dma_start(out=dst, in_=src)], writes=[wslb[i]], dma_key=f"w{i}")
        return i

    gctr = [0]

    def gbank():
        j = gctr[0] % 2
        gctr[0] += 1
        return j

    def inproj_chunk(src_ap, kcn, rhs_fn, rhs_bufs_fn, ntb, bs, evac, bank_fn=gbank):
        i = load_w(src_ap, kcn, "w")
        later = None
        for tb in range(ntb):
            j = bank_fn()

            def fn(e, tb=tb, j=j, i=i):
                r = None
                for kc in range(kcn):
                    r = e.matmul(PS[j][:, 0:bs], lhsT=wsl4[:, i, kc, :], rhs=rhs_fn(kc, tb),
                                 start=(kc == 0), stop=(kc == kcn - 1))
                return r

            P.op("pe", fn, reads=[wslb[i]] + rhs_bufs_fn(tb), writes=[PSb[j]])
            while head_carry:
                head_carry.pop(0)()
            if later is not None:
                later()
            later = evac(tb, j)
        if later is not None:
            later()

    def u_rhs(kc, tb):
        return uT3[:, kc, tb * BS:(tb + 1) * BS]

    def u_bufs(tb):
        return uTb[tb * 4:(tb + 1) * 4]

    P.op("sp", lambda e: [e.dma_start(out=PCOL[:, :], in_=pcol_d[:, :]),
                          e.dma_start(out=LAMV[:, :], in_=lamv_d[:, :])],
         writes=[pcolb, lamvb], dma_key="c0", ninc=2)
    P.op("pool", lambda e: [e.dma_start(out=CID[:, :], in_=cid_d[:, :])], writes=[cidb], dma_key="c1")
    gcol = PCOL[:, 0:16]
    sub_col = PCOL[:, 16:17]
    qg_col = PCOL[:, 17:18]
    kg_col = PCOL[:, 18:19]
    prod = SMALL[:, 0:2]
    ev = SMALL[:, 2:4]
    lam = SMALL[:, 4:5]
    neglam = SMALL[:, 5:6]
    sub08 = SMALL[:, 6:7]
    LPROD = sb("lprod", [128, 128], F32)
    lprodb = Buf("lprod")
    P.op("dve", lambda e: e.tensor_tensor(out=LPROD[:, :], in0=LAMV[:, 0:128], in1=LAMV[:, 128:256], op=ALU.mult),
         reads=[lamvb], writes=[lprodb])
    P.op("dve", lambda e: e.reduce_sum(out=prod, in_=LPROD.rearrange("p (a b) -> p a b", a=2), axis=AX.X),
         reads=[lprodb], writes=[smallb])
    P.op("act", lambda e: e.activation(out=ev, in_=prod, func=AF.Exp), reads=[smallb], pwrites=[smallb])
    P.op("dve", lambda e: e.tensor_tensor(out=lam, in0=SMALL[:, 2:3], in1=SMALL[:, 3:4], op=ALU.subtract),
         reads=[smallb], pwrites=[smallb])
    P.op("dve", lambda e: e.tensor_scalar(out=neglam, in0=lam, scalar1=LAM_INIT, scalar2=-1.0, op0=ALU.add, op1=ALU.mult),
         reads=[smallb], pwrites=[smallb])
    P.op("dve", lambda e: e.tensor_scalar(out=sub08, in0=sub_col, scalar1=1.0 - LAM_INIT, scalar2=None, op0=ALU.mult),
         reads=[pcolb], pwrites=[smallb])

    def phase_0():
        cv = Carver()
        xs = [cv.take(D * 4, F32, f"xs{i}") for i in range(4)]
        xb = [cv.take(D * 2, BF16, f"xb{i}") for i in range(2)]
        junk = cv.take(D * 2, BF16, "junk")
        for nb in range(NB):
            s = nb % 2
            xs_ap, xs_b = xs[nb % 4]
            xb_ap, xb_b = xb[s]
            P.op("sp", lambda e, nb=nb, xs_ap=xs_ap: [e.dma_start(out=xs_ap, in_=x_d[nb * 128:(nb + 1) * 128, :])],
                 writes=[xs_b], dma_key=f"x{nb % 4}")
            P.op("act", lambda e, nb=nb, xs_ap=xs_ap: e.activation(out=junk[0], in_=xs_ap, func=AF.Square,
                                                                     accum_out=SSQ[:, nb:nb + 1]),
                 reads=[xs_b], writes=[junk[1]], pwrites=[ssqb])
            P.op("act", lambda e, nb=nb: e.activation(out=SSQ[:, NB + nb:NB + nb + 1], in_=SSQ[:, nb:nb + 1], func=AF.Ln,
                                                       scale=1.0 / D, bias=EPS),
                 reads=[ssqb], pwrites=[ssqb])
            P.op("act", lambda e, nb=nb: e.activation(out=SSQ[:, 2 * NB + nb:2 * NB + nb + 1],
                                                       in_=SSQ[:, NB + nb:NB + nb + 1], func=AF.Exp, scale=-0.5),
                 reads=[ssqb], pwrites=[ssqb])
            P.op("act", lambda e, nb=nb, xs_ap=xs_ap, xb_ap=xb_ap: e.activation(
                out=xb_ap, in_=xs_ap, func=AF.Copy, scale=SSQ[:, 2 * NB + nb:2 * NB + nb + 1]),
                 reads=[xs_b, ssqb], writes=[xb_b])
            for g4 in range(4):
                j = gbank()

                def fn(e, g4=g4, j=j, xb_ap=xb_ap):
                    r = None
                    for i in range(4):
                        kc = g4 * 4 + i
                        r = e.transpose(psbf(j, 512)[:, i * 128:(i + 1) * 128], xb_ap[:, kc * 128:(kc + 1) * 128], ident)
                    return r

                P.op("pe", fn, reads=[xb_b, cidb], writes=[PSb[j]])
                P.op("dve", lambda e, g4=g4, j=j, nb=nb: e.tensor_tensor(
                    out=uT3[:, g4 * 4:(g4 + 1) * 4, nb * 128:(nb + 1) * 128],
                    in0=psbf(j, 512).rearrange("p (a b) -> p a b", a=4),
                    in1=gcol[:, g4 * 4:(g4 + 1) * 4].unsqueeze(2).to_broadcast([128, 4, 128]), op=ALU.mult),
                     reads=[PSb[j], pcolb], pwrites=[uTb[nb]])
        P.fence()

    def evac_copy_act(dst_ap, dst_buf, bs):
        def ev_(tb, j):
            P.op("act", lambda e: e.activation(out=dst_ap[:, tb * bs:(tb + 1) * bs], in_=PS[j][:, 0:bs], func=AF.Copy),
                 reads=[PSb[j]], pwrites=[dst_buf])
        return ev_

    def evac_copy_dve(dst_ap, dst_buf, bs):
        def ev_(tb, j):
            P.op("dve", lambda e: e.tensor_copy(out=dst_ap[:, tb * bs:(tb + 1) * bs], in_=PS[j][:, 0:bs]),
                 reads=[PSb[j]], pwrites=[dst_buf])
        return ev_

    def evac_silu(dst_ap, dst_buf, etmp):
        def ev_(tb, j):
            t_ap, t_b = etmp[tb % 2]
            P.op("act", lambda e: e.activation(out=t_ap, in_=PS[j][:, :], func=AF.Exp, scale=-1.0),
                 reads=[PSb[j]], writes=[t_b])
            P.op("act", lambda e: e.activation(out=t_ap, in_=t_ap, func=AF.Ln, bias=1.0), reads=[t_b], writes=[t_b])
            P.op("act", lambda e: e.activation(out=t_ap, in_=t_ap, func=AF.Exp, scale=-1.0), reads=[t_b], writes=[t_b])
            P.op("dve", lambda e: e.tensor_tensor(out=dst_ap[:, tb * BS:(tb + 1) * BS], in0=PS[j][:, :], in1=t_ap, op=ALU.mult),
                 reads=[PSb[j], t_b], pwrites=[dst_buf])
        return ev_

    def v_transposes(vT_ap, vT_b, v_ap, v_b):
        for tb in range(TB):
            j = gbank()

            def fn(e, tb=tb, j=j):
                r = None
                for i in range(4):
                    t0 = tb * BS + i * 128
                    r = e.transpose(psbf(j, 512)[:, i * 128:(i + 1) * 128], vT_ap[:, t0:t0 + 128], ident)
                return r

            P.op("pe", fn, reads=[vT_b, cidb], writes=[PSb[j]])
            P.op("dve", lambda e, tb=tb, j=j: e.tensor_copy(out=v_ap[:, tb * BS:(tb + 1) * BS], in_=psbf(j, 512)),
                 reads=[PSb[j]], pwrites=[v_b])

    SC = [2, 3]
    SC4 = [2, 3, 5, 7]
    PVB = [4, 5]
    SMB = [6, 7]

    def phase_A():
        cv = Carver()
        QX = [cv.take(S * 2, BF16, f"QX{m}") for m in range(2)]
        KX = [cv.take(S * 2, BF16, f"KX{m}") for m in range(2)]
        vv, vvb = cv.take(S * 2, BF16, "v")
        sg, sgb = cv.take(S * 2, BF16, "sg")
        PP = [cv.take(2 * BS * 2, BF16, f"PP{i}") for i in range(2)]
        pairctr = [0]

        def next_pair():
            p = pairctr[0] % 2
            pairctr[0] += 1
            return (2, 3) if p == 0 else (0, 1)

        def pair_ap(b0):
            return PST[0][:, b0 * 512:(b0 + 2) * 512]
        off_T = cv.off
        T = [cv.take(BS * 4, F32, f"T{i}") for i in range(8)]
        sq = cv.take(BS * 2, BF16, "sq")
        r1, r2, t1, t2, rs = T[0], T[1], T[2], T[3], T[4]
        etmp = [T[6], T[7]]
        SQ = SCR[:, off_T:off_T + S * 2].bitcast(BF16)
        SK = SCR[:, off_T + 4096:off_T + 4096 + S * 2].bitcast(BF16)
        vT = SCR[:, off_T + 8192:off_T + 8192 + S * 2].bitcast(BF16)
        sqbufs = [T[0][1], T[1][1]]
        skbufs = [T[2][1], T[3][1]]
        vTbufs = [T[4][1], T[5][1]]
        P.op("pool", lambda e: [e.dma_start(out=QX[0][0], in_=qaug_d[:, :]), e.dma_start(out=QX[1][0], in_=qaug_d[:, :]),
                                e.dma_start(out=KX[0][0], in_=kaug_d[:, :]), e.dma_start(out=KX[1][0], in_=kaug_d[:, :])],
             writes=[QX[0][1], QX[1][1], KX[0][1], KX[1][1]], dma_key="c5", ninc=4)

        def evac_stage(dst_ap, bufs, eng, scale):
            def ev_(tb, j):
                kw = dict(writes=bufs) if tb == 0 else dict(pwrites=bufs)
                if eng == "act":
                    P.op("act", lambda e: e.activation(out=dst_ap[:, tb * BS:(tb + 1) * BS], in_=PS[j][:, :], func=AF.Copy,
                                                       scale=float(scale)), reads=[PSb[j]], **kw)
                else:
                    P.op("dve", lambda e: e.tensor_copy(out=dst_ap[:, tb * BS:(tb + 1) * BS], in_=PS[j][:, :]),
                         reads=[PSb[j]], **kw)
            return ev_

        def v_transposes_a():
            for tb in range(TB):
                j = gbank()

                def fn(e, tb=tb, j=j):
                    r = None
                    for i in range(4):
                        t0 = tb * BS + i * 128
                        r = e.transpose(psbf(j, 512)[:, i * 128:(i + 1) * 128], vT[:, t0:t0 + 128], ident)
                    return r

                P.op("pe", fn, reads=vTbufs + [cidb], writes=[PSb[j]])
                P.op("dve", lambda e, tb=tb, j=j: e.tensor_copy(out=vv[:, tb * BS:(tb + 1) * BS], in_=psbf(j, 512)),
                     reads=[PSb[j]], pwrites=[vvb])

        for h in range(8):
            slope = 2.0 ** (-(h + 1))
            qscale = 2.0 ** (h - 2)
            inproj_chunk(win_d[:, OFF_AQ + h * 128:OFF_AQ + (h + 1) * 128], KC, u_rhs, u_bufs, TB, BS,
                         evac_stage(SQ, sqbufs, "act", qscale))
            inproj_chunk(win_d[:, OFF_AK + h * 128:OFF_AK + (h + 1) * 128], KC, u_rhs, u_bufs, TB, BS,
                         evac_stage(SK, skbufs, "dve", 1.0))
            P.op("sp", lambda e: [e.dma_start(out=QX[0][0][0:64, :], in_=SQ[0:64, :]),
                                  e.dma_start(out=QX[1][0][0:64, :], in_=SQ[64:128, :])],
                 reads=sqbufs, writes=[QX[0][1], QX[1][1]], dma_key="qx", ninc=2)
            P.op("sp", lambda e: [e.dma_start(out=KX[0][0][0:64, :], in_=SK[0:64, :]),
                                  e.dma_start(out=KX[1][0][0:64, :], in_=SK[64:128, :])],
                 reads=skbufs, writes=[KX[0][1], KX[1][1]], dma_key="kx", ninc=2)
            inproj_chunk(win_d[:, OFF_AV + h * 128:OFF_AV + (h + 1) * 128], KC, u_rhs, u_bufs, TB, BS,
                         evac_stage(vT, vTbufs, "act", 1.0))
            v_transposes_a()
            inproj_chunk(win_d[:, OFF_AG + h * 128:OFF_AG + (h + 1) * 128], KC, u_rhs, u_bufs, TB, BS,
                         evac_silu(sg, sgb, etmp))

            tiles = [(qb, kb) for qb in range(TB) for kb in range(NB)]

            def scores(i, qb, kb, h=h, slope=slope):
                delta = qb * BS - kb * 128
                if delta >= 128:
                    regions = [(0, BS, 128, False)]
                elif delta <= -BS:
                    regions = [(0, BS, 96, False)]
                else:
                    sidx = (-delta) // 128
                    regions = []
                    if sidx > 0:
                        regions.append((0, sidx * 128, 96, False))
                    regions.append((sidx * 128, (sidx + 1) * 128, 64, True))
                    if sidx < 3:
                        regions.append(((sidx + 1) * 128, BS, 128, False))
                b0, b1 = next_pair()
                for m, j in ((0, b0), (1, b1)):
                    def fn(e, m=m, j=j):
                        r = None
                        for (c0, c1, kw_, dg) in regions:
                            r = e.matmul(PS[j][:, c0:c1], lhsT=KX[m][0][0:kw_, kb * 128:(kb + 1) * 128],
                                         rhs=QX[m][0][0:kw_, qb * BS + c0:qb * BS + c1], start=True, stop=not dg)
                            if dg:
                                r = e.matmul(PS[j][:, c0:c1], lhsT=ident, rhs=diagm, start=False, stop=True)
                        return r
                    P.op("pe", fn, reads=[KX[m][1], QX[m][1], cidb], writes=[PSb[j]])
                pp_ap, pp_b = PP[i % 2]
                P.op("act", lambda e: e.activation(out=pp_ap, in_=pair_ap(b0), func=AF.Exp, scale=float(slope)),
                     reads=[PSb[b0], PSb[b1]], writes=[pp_b])

            def pvsm(i, qb, kb):
                pp_ap, pp_b = PP[i % 2]
                for m in range(2):
                    p_ap = pp_ap[:, m * BS:(m + 1) * BS]
                    st, sp_ = (kb == 0), (kb == NB - 1)
                    kw = dict(writes=[PSb[PVB[m]]]) if st else dict(pwrites=[PSb[PVB[m]]])
                    P.op("pe", lambda e, m=m, p_ap=p_ap, st=st, sp_=sp_: e.matmul(
                        PS[PVB[m]][:, :], lhsT=vv[:, kb * 128:(kb + 1) * 128], rhs=p_ap, start=st, stop=sp_),
                         reads=[vvb, pp_b], **kw)
                    kw = dict(writes=[PSb[SMB[m]]]) if st else dict(pwrites=[PSb[SMB[m]]])
                    P.op("pe", lambda e, m=m, p_ap=p_ap, st=st, sp_=sp_: e.matmul(
                        PS[SMB[m]][:, :], lhsT=ones, rhs=p_ap, start=st, stop=sp_),
                         reads=[cidb, pp_b], **kw)

            def epi1(qb):
                for rr, m_ in ((r1, 0), (r2, 1)):
                    P.op("act", lambda e, rr=rr, m_=m_: e.activation(out=rr[0], in_=PS[SMB[m_]][:, :], func=AF.Ln),
                         reads=[PSb[SMB[m_]]], writes=[rr[1]])
                    P.op("act", lambda e, rr=rr: e.activation(out=rr[0], in_=rr[0], func=AF.Exp, scale=-1.0),
                         reads=[rr[1]], writes=[rr[1]])
                P.op("dve", lambda e: e.tensor_copy(out=t1[0], in_=PS[PVB[0]][:, :]), reads=[PSb[PVB[0]]], writes=[t1[1]])
                P.op("dve", lambda e: e.tensor_copy(out=t2[0], in_=PS[PVB[1]][:, :]), reads=[PSb[PVB[1]]], writes=[t2[1]])
                P.op("dve", lambda e: e.tensor_tensor(out=t1[0], in0=t1[0], in1=r1[0], op=ALU.mult),
                     reads=[t1[1], r1[1]], writes=[t1[1]])
                P.op("dve", lambda e: e.tensor_tensor(out=t2[0], in0=t2[0], in1=r2[0], op=ALU.mult),
                     reads=[t2[1], r2[1]], writes=[t2[1]])
                P.op("dve", lambda e: e.scalar_tensor_tensor(out=r1[0], in0=t2[0], scalar=neglam, in1=t1[0],
                                                              op0=ALU.mult, op1=ALU.add),
                     reads=[t1[1], t2[1], smallb], writes=[r1[1]])
                P.op("dve", lambda e: e.tensor_tensor(out=sq[0], in0=r1[0], in1=r1[0], op=ALU.mult),
                     reads=[r1[1]], writes=[sq[1]])

            def epi2(qb, h=h, bank=None):
                j = next_pair()[0] if bank is None else bank
                P.op("pe", lambda e, j=j: e.matmul(PS[j][:, :], lhsT=ones, rhs=sq[0], start=True, stop=True),
                     reads=[cidb, sq[1]], writes=[PSb[j]])
                P.op("act", lambda e, j=j: e.activation(out=rs[0], in_=PS[j][:, :], func=AF.Ln, scale=1.0 / 128, bias=EPS),
                     reads=[PSb[j]], writes=[rs[1]])
                P.op("act", lambda e: e.activation(out=rs[0], in_=rs[0], func=AF.Exp, scale=-0.5),
                     reads=[rs[1]], writes=[rs[1]])
                P.op("dve", lambda e: e.tensor_tensor(out=t1[0], in0=r1[0], in1=rs[0], op=ALU.mult),
                     reads=[r1[1], rs[1]], writes=[t1[1]])
                P.op("dve", lambda e: e.scalar_tensor_tensor(
                    out=oag3[:, h, qb * BS:(qb + 1) * BS], in0=t1[0], scalar=sub08, in1=sg[:, qb * BS:(qb + 1) * BS],
                    op0=ALU.mult, op1=ALU.mult),
                     reads=[t1[1], smallb, sgb], pwrites=[oagb[h]])

            pending = None
            for i, (qb, kb) in enumerate(tiles):
                scores(i, qb, kb)
                if i > 0:
                    pq, pk = tiles[i - 1]
                    pvsm(i - 1, pq, pk)
                    if pk == NB - 1:
                        epi1(pq)
                        pending = (pq, i + 4)
                if pending is not None and i >= pending[1]:
                    epi2(pending[0])
                    pending = None
            pq, pk = tiles[-1]
            pvsm(len(tiles) - 1, pq, pk)
            if pending is not None:
                epi2(pending[0])
                pending = None
            epi1(pq)
            if h < 7:
                head_carry.append(lambda pq=pq, epi2=epi2: epi2(pq, bank=SMB[0]))
            else:
                epi2(pq)
        if dbg is not None:
            dbg_outs.append(("dbg_oag", oag, [128, 8 * S], BF16, oagb))
        P.fence()

    def phase_B():
        cv = Carver()
        qT, qTb = cv.take(S * 2, BF16, "qT")
        kT, kTb = cv.take(S * 2, BF16, "kT")
        vT, vTb = cv.take(S * 2, BF16, "vT")
        vv, vvb = cv.take(S * 2, BF16, "v")
        sg, sgb = cv.take(S * 2, BF16, "sg")
        ROPE, ropeb = cv.take(256 * 4, F32, "rope")
        rope3 = ROPE.rearrange("p (a b) -> p a b", a=4)
        PP = [cv.take(2 * BS * 2, BF16, f"PP{i}") for i in range(2)]
        etmp = [cv.take(BS * 4, F32, f"etmp{i}") for i in range(2)]
        raw = [cv.take(BS * 4, F32, f"raw{i}") for i in range(2)]
        kp = [cv.take(BS * 2, BF16, f"kp{i}") for i in range(2)]
        sqb = [cv.take(BS * 2, BF16, f"sq{i}") for i in range(2)]
        rsb = [cv.take(BS * 4, F32, f"rs{i}") for i in range(2)]
        ra = cv.take(BS * 4, F32, "ra")
        rb = cv.take(BS * 4, F32, "rb")
        r1 = cv.take(BS * 4, F32, "r1")
        t1 = cv.take(BS * 4, F32, "t1")
        P.op("sp", lambda e: [e.dma_start(out=ROPE, in_=crope_d[:, :])], writes=[ropeb], dma_key="c3")
        nrow = BS // GRID_W

        def evac_normrope(dst_ap, dst_buf, gcol_ap):
            def ev_(tb, j):
                s = tb % 2
                P.op("act", lambda e: e.activation(out=kp[s][0], in_=PS[j][:, :], func=AF.Copy, scale=gcol_ap),
                     reads=[PSb[j], pcolb], writes=[kp[s][1]])
                P.op("act", lambda e: e.activation(out=sqb[s][0], in_=PS[j][:, :], func=AF.Square),
                     reads=[PSb[j]], writes=[sqb[s][1]])
                return lambda: part2(tb, s)

            def part2(tb, s):
                P.op("pe", lambda e: e.matmul(PS[SC[0]][:, :], lhsT=ones, rhs=sqb[s][0], start=True, stop=True),
                     reads=[cidb, sqb[s][1]], writes=[PSb[SC[0]]])
                P.op("pe", lambda e: e.matmul(PS[SC[1]][:, :], lhsT=swapm, rhs=kp[s][0], start=True, stop=True),
                     reads=[cidb, kp[s][1]], writes=[PSb[SC[1]]])
                P.op("act", lambda e: e.activation(out=rsb[s][0], in_=PS[SC[0]][:, :], func=AF.Ln, scale=1.0 / 128, bias=EPS),
                     reads=[PSb[SC[0]]], writes=[rsb[s][1]])
                P.op("act", lambda e: e.activation(out=rsb[s][0], in_=rsb[s][0], func=AF.Exp, scale=-0.5),
                     reads=[rsb[s][1]], writes=[rsb[s][1]])
                r0 = tb * nrow
                for half in range(2):
                    ps_ = slice(64 * half, 64 * half + 64)
                    if half == 0:
                        ct = rope3[0:64, 0, r0:r0 + nrow].unsqueeze(2).to_broadcast([64, nrow, 64])
                        st = rope3[0:64, 1, r0:r0 + nrow].unsqueeze(2).to_broadcast([64, nrow, 64])
                    else:
                        ct = rope3[64:128, 2, :].unsqueeze(1).to_broadcast([64, nrow, 64])
                        st = rope3[64:128, 3, :].unsqueeze(1).to_broadcast([64, nrow, 64])
                    P.op("dve", lambda e, ps_=ps_, ct=ct: e.tensor_tensor(
                        out=ra[0][ps_, :].rearrange("p (a b) -> p a b", b=64),
                        in0=kp[s][0][ps_, :].rearrange("p (a b) -> p a b", b=64), in1=ct, op=ALU.mult),
                         reads=[kp[s][1], ropeb], pwrites=[ra[1]])
                    P.op("dve", lambda e, ps_=ps_, st=st: e.tensor_tensor(
                        out=rb[0][ps_, :].rearrange("p (a b) -> p a b", b=64),
                        in0=PS[SC[1]][ps_, :].rearrange("p (a b) -> p a b", b=64), in1=st, op=ALU.mult),
                         reads=[PSb[SC[1]], ropeb], pwrites=[rb[1]])
                P.op("dve", lambda e: e.tensor_tensor(out=ra[0], in0=ra[0], in1=rb[0], op=ALU.add),
                     reads=[ra[1], rb[1]], writes=[ra[1]])
                P.op("dve", lambda e: e.tensor_tensor(out=dst_ap[:, tb * BS:(tb + 1) * BS], in0=ra[0], in1=rsb[s][0], op=ALU.mult),
                     reads=[ra[1], rsb[s][1]], pwrites=[dst_buf])
            return ev_

        scale_b = 128.0 ** -0.5
        for gk in range(2):
            inproj_chunk(win_d[:, OFF_BK + gk * 128:OFF_BK + (gk + 1) * 128], KC, u_rhs, u_bufs, TB, BS,
                         evac_normrope(kT, kTb, kg_col))
            inproj_chunk(win_d[:, OFF_BV + gk * 128:OFF_BV + (gk + 1) * 128], KC, u_rhs, u_bufs, TB, BS,
                         evac_copy_act(vT, vTb, BS))
            v_transposes(vT, vTb, vv, vvb)
            for hq in range(4):
                h = gk * 4 + hq
                inproj_chunk(win_d[:, OFF_BQ + h * 128:OFF_BQ + (h + 1) * 128], KC, u_rhs, u_bufs, TB, BS,
                             evac_normrope(qT, qTb, qg_col))
                inproj_chunk(win_d[:, OFF_BG + h * 128:OFF_BG + (h + 1) * 128], KC, u_rhs, u_bufs, TB, BS,
                             evac_silu(sg, sgb, etmp))
                tiles = [(qb, kb) for qb in range(TB) for kb in range(NB)]

                def scores_pair(p):
                    b0, b1 = (2, 3) if p % 2 == 0 else (0, 1)
                    for t_, j in ((2 * p, b0), (2 * p + 1, b1)):
                        qb, kb = tiles[t_]
                        P.op("pe", lambda e, j=j, qb=qb, kb=kb: e.matmul(
                            PS[j][:, :], lhsT=kT[:, kb * 128:(kb + 1) * 128], rhs=qT[:, qb * BS:(qb + 1) * BS],
                            start=True, stop=True), reads=[kTb, qTb], writes=[PSb[j]])
                    pp_ap, pp_b = PP[p % 2]
                    P.op("act", lambda e: e.activation(out=pp_ap, in_=PST[0][:, b0 * 512:(b0 + 2) * 512], func=AF.Exp,
                                                       scale=scale_b), reads=[PSb[b0], PSb[b1]], writes=[pp_b])

                def pvsm(t_):
                    qb, kb = tiles[t_]
                    pp_ap, pp_b = PP[(t_ // 2) % 2]
                    p_ap = pp_ap[:, (t_ % 2) * BS:(t_ % 2 + 1) * BS]
                    pvb, smb = PVB[qb % 2], SMB[qb % 2]
                    st, sp_ = (kb == 0), (kb == NB - 1)
                    kw = dict(writes=[PSb[pvb]]) if st else dict(pwrites=[PSb[pvb]])
                    P.op("pe", lambda e: e.matmul(PS[pvb][:, :], lhsT=vv[:, kb * 128:(kb + 1) * 128], rhs=p_ap,
                                                  start=st, stop=sp_), reads=[vvb, pp_b], **kw)
                    kw = dict(writes=[PSb[smb]]) if st else dict(pwrites=[PSb[smb]])
                    P.op("pe", lambda e: e.matmul(PS[smb][:, :], lhsT=ones, rhs=p_ap, start=st, stop=sp_),
                         reads=[cidb, pp_b], **kw)

                def epilogue(qb, h=h):
                    pvb, smb = PVB[qb % 2], SMB[qb % 2]
                    P.op("act", lambda e: e.activation(out=r1[0], in_=PS[smb][:, :], func=AF.Ln),
                         reads=[PSb[smb]], writes=[r1[1]])
                    P.op("act", lambda e: e.activation(out=r1[0], in_=r1[0], func=AF.Exp, scale=-1.0),
                         reads=[r1[1]], writes=[r1[1]])
                    P.op("dve", lambda e: e.tensor_tensor(out=t1[0], in0=PS[pvb][:, :], in1=r1[0], op=ALU.mult),
                         reads=[PSb[pvb], r1[1]], writes=[t1[1]])
                    P.op("dve", lambda e: e.tensor_tensor(out=obg3[:, h, qb * BS:(qb + 1) * BS], in0=t1[0],
                                                          in1=sg[:, qb * BS:(qb + 1) * BS], op=ALU.mult),
                         reads=[t1[1], sgb], pwrites=[obgb[h]])

                npair = len(tiles) // 2
                for p in range(npair + 1):
                    if p < npair:
                        scores_pair(p)
                    if p >= 1:
                        for t_ in (2 * p - 2, 2 * p - 1):
                            pvsm(t_)
                            if tiles[t_][1] == NB - 1:
                                epilogue(tiles[t_][0])
        if dbg is not None:
            dbg_outs.append(("dbg_obg", obg, [128, 8 * S], BF16, obgb))
        P.fence()

    def phase_E():
        cv = Carver()
        mT0, mT0b = cv.take(KC * HS * 2, BF16, "mT0")
        mT0_3 = mT0.rearrange("p (k s) -> p k s", k=KC)
        off_after_m = cv.off
        sa = [cv.take(EBS * 4, F32, f"sa{i}") for i in range(2)]
        sbt = [cv.take(EBS * 4, F32, f"sb{i}") for i in range(2)]
        tt = [cv.take(EBS * 4, F32, f"tt{i}") for i in range(2)]
        ectr = [0]

        for H in range(2):
            tok0 = H * HS
            for c in range(16):
                iw = [load_w(win_d[:, OFF_GM + c * 128:OFF_GM + (c + 1) * 128], KC, "gma"),
                      load_w(win_d[:, OFF_GM + D + c * 128:OFF_GM + D + (c + 1) * 128], KC, "gmb"),
                      load_w(wpa_d[:, c * 128:(c + 1) * 128], 8, "wpa"),
                      load_w(wpb_d[:, c * 128:(c + 1) * 128], 8, "wpb")]
                for tb in range(ETB):
                    t0 = tok0 + tb * EBS
                    st_ = ectr[0] % 2
                    ectr[0] += 1
                    banks = [4 * st_ + q for q in range(4)]
                    ub = uTb[t0 // 128:(t0 + EBS) // 128]

                    def mk(slot, kcn, rhs3, rbufs, bank, t0):
                        def fn(e):
                            r = None
                            for kc in range(kcn):
                                r = e.matmul(PS[bank][:, 0:EBS], lhsT=wsl4[:, slot, kc, :], rhs=rhs3[:, kc, t0:t0 + EBS],
                                             start=(kc == 0), stop=(kc == kcn - 1))
                            return r
                        P.op("pe", fn, reads=[wslb[slot]] + list(rbufs), writes=[PSb[bank]])

                    mk(iw[0], KC, uT3, ub, banks[0], t0)
                    mk(iw[1], KC, uT3, ub, banks[1], t0)
                    mk(iw[2], 8, oag3, oagb, banks[2], t0)
                    mk(iw[3], 8, obg3, obgb, banks[3], t0)
                    P.op("act", lambda e, st_=st_, b=banks[0]: e.activation(out=sa[st_][0], in_=PS[b][:, 0:EBS], func=AF.Sigmoid),
                         reads=[PSb[banks[0]]], writes=[sa[st_][1]])
                    P.op("act", lambda e, st_=st_, b=banks[1]: e.activation(out=sbt[st_][0], in_=PS[b][:, 0:EBS], func=AF.Sigmoid),
                         reads=[PSb[banks[1]]], writes=[sbt[st_][1]])
                    P.op("dve", lambda e, st_=st_, b=banks[2]: e.tensor_tensor(out=sa[st_][0], in0=PS[b][:, 0:EBS], in1=sa[st_][0], op=ALU.mult),
                         reads=[PSb[banks[2]], sa[st_][1]], writes=[sa[st_][1]])
                    P.op("dve", lambda e, st_=st_, b=banks[3]: e.tensor_tensor(out=sbt[st_][0], in0=PS[b][:, 0:EBS], in1=sbt[st_][0], op=ALU.mult),
                         reads=[PSb[banks[3]], sbt[st_][1]], writes=[sbt[st_][1]])
                    if H == 0:
                        dst = mT0_3[:, c, tb * EBS:(tb + 1) * EBS]
                        kw = dict(pwrites=[mT0b])
                    else:
                        dst = uT3[:, c, tb * EBS:(tb + 1) * EBS]
                        kw = dict(pwrites=uTb[(tb * EBS) // 128:((tb + 1) * EBS) // 128])
                    P.op("dve", lambda e, st_=st_, dst=dst: e.tensor_tensor(out=dst, in0=sa[st_][0], in1=sbt[st_][0], op=ALU.add),
                         reads=[sa[st_][1], sbt[st_][1]], **kw)

        if S == 2048:
            def wo(kc):
                return (oag3 if kc < 8 else obg3)[:, kc % 8, :]
            wo_bufs = oagb + obgb
        else:
            WOUT3 = WOUT.rearrange("p (k c) -> p k c", k=KC)
            woutb = Buf("wout")

            def wo(kc):
                return WOUT3[:, kc, :]
            wo_bufs = [woutb]
        P.op("pool", lambda e: [e.dma_start(out=wo(kc), in_=wout_d[kc * 128:(kc + 1) * 128, :]) for kc in range(KC)],
             writes=wo_bufs, dma_key="wout", ninc=KC)
        FG = wsl[:, 0:2 * D].bitcast(F32)
        JK = wsl[:, 2 * D:3 * D]
        P.op("sp", lambda e: [e.dma_start(out=FG, in_=fg_d[:, :])], writes=[wslb[0], wslb[1]], dma_key="c4")
        P.fence()
        cv.off = off_after_m
        xh = [cv.take(D * 4, F32, f"xh{i}") for i in range(2)]
        uhalf = uTb[0:HS // 128]
        for ts in range(NB):
            st_ = ts % 2
            tloc = (ts * 128) % HS
            if ts * 128 < HS:
                mt3, mbufs = mT0_3, [mT0b]
            else:
                mt3, mbufs = uT3, uhalf
            for ob in range(4):
                bank = 4 * st_ + ob

                def fn(e, bank=bank, ob=ob, mt3=mt3, tloc=tloc):
                    r = None
                    for kc in range(KC):
                        r = e.matmul(PS[bank][:, :], lhsT=mt3[:, kc, tloc:tloc + 128], rhs=wo(kc)[:, ob * 512:(ob + 1) * 512],
                                     start=(kc == 0), stop=(kc == KC - 1))
                    return r
                P.op("pe", fn, reads=mbufs + wo_bufs, writes=[PSb[bank]])
            xh_ap, xh_b = xh[st_]
            P.op("sp", lambda e, ts=ts, xh_ap=xh_ap: [e.dma_start(out=xh_ap, in_=x_d[ts * 128:(ts + 1) * 128, :])],
                 writes=[xh_b], dma_key=f"x{st_}")
            for q in range(4):
                P.op("dve", lambda e, st_=st_, xh_ap=xh_ap, q=q: e.tensor_tensor(
                    out=xh_ap[:, q * 512:(q + 1) * 512], in0=PS[4 * st_ + q][:, :], in1=xh_ap[:, q * 512:(q + 1) * 512], op=ALU.add),
                     reads=[PSb[4 * st_ + q], xh_b], writes=[xh_b])
            P.op("act", lambda e, ts=ts, xh_ap=xh_ap: e.activation(out=JK, in_=xh_ap, func=AF.Square,
                                                                     accum_out=SSQ[:, 3 * NB + ts:3 * NB + ts + 1]),
                 reads=[xh_b], writes=[wslb[2]], pwrites=[ssqb])
            P.op("act", lambda e, ts=ts: e.activation(out=SSQ[:, NB + ts:NB + ts + 1], in_=SSQ[:, 3 * NB + ts:3 * NB + ts + 1],
                                                       func=AF.Ln, scale=1.0 / D, bias=EPS),
                 reads=[ssqb], pwrites=[ssqb])
            P.op("act", lambda e, ts=ts: e.activation(out=SSQ[:, 2 * NB + ts:2 * NB + ts + 1], in_=SSQ[:, NB + ts:NB + ts + 1],
                                                       func=AF.Exp, scale=-0.5),
                 reads=[ssqb], pwrites=[ssqb])
            P.op("dve", lambda e, ts=ts, xh_ap=xh_ap: e.scalar_tensor_tensor(
                out=xh_ap, in0=xh_ap, scalar=SSQ[:, 2 * NB + ts:2 * NB + ts + 1], in1=FG, op0=ALU.mult, op1=ALU.mult),
                 reads=[xh_b, ssqb, wslb[0], wslb[1]], writes=[xh_b])
            P.op("sp", lambda e, ts=ts, xh_ap=xh_ap: [e.dma_start(out=out_d[ts * 128:(ts + 1) * 128, :], in_=xh_ap)],
                 reads=[xh_b], dma_key=f"o{st_}")
        last = [i for i in P.streams["sp"] if i.dma and i.key in ("o0", "o1")][-2:]
        fin = P.op("sp", lambda e: e.nop(), reads=[xh[0][1], xh[1][1]], writes=[xh[0][1], xh[1][1]])

    if "p0" in phases:
        phase_0()
    if "A" in phases:
        phase_A()
    if "B" in phases:
        phase_B()
    if "E" in phases:
        phase_E()

    dbg_aps = {}
    for (name, ap, shape, dt, bufs) in dbg_outs:
        d = nc.dram_tensor(name, list(shape), dt, kind="ExternalOutput").ap()
        dbg_aps[name] = d
        P.op("sp", lambda e, d=d, ap=ap: [e.dma_start(out=d[:, :], in_=ap)], reads=list(bufs), dma_key="dbg_" + name)
    if dbg_outs:
        dl = [i for i in P.streams["sp"] if i.dma and i.key.startswith("dbg_")]
        fb = Buf("fin")
        for i in dl:
            fb.wd.append(i)
        P.op("sp", lambda e: e.nop(), reads=[fb])
    if "E" not in phases:
        pass

    P.finalize(nc, stack)
    with nc.Block() as block:
        @block.tensor
        def _(e):
            P.emit("pe", e)

        @block.scalar
        def _(e):
            P.emit("act", e)

        @block.vector
        def _(e):
            P.emit("dve", e)

        @block.gpsimd
        def _(e):
            P.emit("pool", e)

        @block.sync
        def _(e):
            P.emit("sp", e)
    stack.close()
    return nc


def make_consts(S):
    p = np.arange(128, dtype=np.float64)[:, None]
    ident = np.eye(128, dtype=np.float32)
    ones = np.ones((128, 128), np.float32)
    swap = np.zeros((128, 128), np.float32)
    for i in range(128):
        sec, r = divmod(i, 64)
        partner = sec * 64 + (r + 32) % 64
        swap[i, partner] = 1.0
    pj = np.arange(128, dtype=np.float32)
    diag = -np.abs(pj[None, :] - pj[:, None])
    cident = np.concatenate([ident, ones, swap, diag], axis=1).astype(np.float32)
    j = np.arange(512, dtype=np.float64)[None, :]
    B0 = (j - p)
    c = np.arange(896, dtype=np.float64)[None, :]
    Bm = np.abs(c - p - 384.0)
    Bhi = 256.0 * np.floor(Bm / 256.0)
    cbias = np.concatenate([-Bhi, -(Bm - Bhi)], axis=1).astype(np.float32)
    n = 32
    inv = (10000.0 ** (-np.arange(0, 64, 2, dtype=np.float32) / 64.0)).astype(np.float32)
    d = np.arange(64)
    invd = inv[d % 32]
    sign = np.where(d < 32, -1.0, 1.0).astype(np.float32)
    rows = np.arange(32, dtype=np.float32)
    cols = np.arange(64, dtype=np.float32)
    ang_r = (rows[None, :] * invd[:, None]).astype(np.float32)
    ang_c = (cols[None, :] * invd[:, None]).astype(np.float32)
    crope = np.zeros((128, 4, 64), np.float32)
    crope[0:64, 0, 0:32] = np.cos(ang_r)
    crope[0:64, 1, 0:32] = np.sin(ang_r) * sign[:, None]
    crope[64:128, 2, :] = np.cos(ang_c)
    crope[64:128, 3, :] = np.sin(ang_c) * sign[:, None]
    nd = (2 * S - 128 - 512) // 128 + 1
    abt = np.zeros((128, 8 * nd), np.float32)
    for h in range(8):
        slope = 2.0 ** (-(h + 1))
        for di in range(nd):
            delta = di * 128 - (S - 128)
            pp = np.arange(128, dtype=np.float64)
            if delta >= 128:
                abt[:, h * nd + di] = slope * (pp - delta)
            elif delta <= -512:
                abt[:, h * nd + di] = slope * (delta - pp)
    t = np.arange(S)
    hi = (256 * (t // 256)).astype(np.float32)
    lo = (t % 256).astype(np.float32)
    one = np.ones(S, np.float32)
    qaug = np.zeros((128, S), np.float32)
    kaug = np.zeros((128, S), np.float32)
    qaug[64], qaug[65], qaug[66], qaug[67] = hi, lo, -one, -one
    qaug[96], qaug[97], qaug[98], qaug[99] = -2 * hi, -2 * lo, 2 * one, 2 * one
    for r0 in (64, 96):
        kaug[r0], kaug[r0 + 1], kaug[r0 + 2], kaug[r0 + 3] = one, one, hi, lo
    return cident, cbias, crope.reshape(128, 256), abt, qaug, kaug


def make_in_maps(x, norm_g, w_in, a_lambda_q1, a_lambda_k1, a_lambda_q2, a_lambda_k2, a_subln_g, b_qnorm_g,
                 b_knorm_g, w_proj_a, w_proj_b, w_out, final_g, S):
    f = lambda a: np.ascontiguousarray(np.asarray(a, dtype=np.float32))
    cident, cbias, crope, abt, qaug, kaug = make_consts(S)
    pcol = np.zeros((128, 19), np.float32)
    pcol[:, 0:16] = f(norm_g).reshape(16, 128).T
    pcol[:, 16] = f(a_subln_g).reshape(128)
    pcol[:, 17] = f(b_qnorm_g).reshape(128)
    pcol[:, 18] = f(b_knorm_g).reshape(128)
    lamrow = np.concatenate([f(a_lambda_q1).reshape(64), f(a_lambda_q2).reshape(64),
                             f(a_lambda_k1).reshape(64), f(a_lambda_k2).reshape(64)])
    lamv = np.ascontiguousarray(np.broadcast_to(lamrow[None, :], (128, 256)))
    fgb = np.ascontiguousarray(np.broadcast_to(f(final_g).reshape(1, D), (128, D)))
    shared = {
        "w_in": f(w_in).reshape(D, INC), "w_proj_a": f(w_proj_a).reshape(1024, D),
        "w_proj_b": f(w_proj_b).reshape(1024, D), "w_out": f(w_out).reshape(D, D),
        "pcol": pcol, "lamv": lamv, "fgb": fgb, "cident": cident, "crope": crope, "qaug": qaug, "kaug": kaug,
    }
    xx = f(x)
    maps = []
    for b in range(xx.shape[0]):
        m = dict(shared)
        m["x"] = np.ascontiguousarray(xx[b])
        maps.append(m)
    return maps


_NC_CACHE = {}


def kernel(x, norm_g, w_in, a_lambda_q1, a_lambda_k1, a_lambda_q2, a_lambda_k2, a_subln_g, b_qnorm_g,
           b_knorm_g, w_proj_a, w_proj_b, w_out, final_g):
    x = np.asarray(x)
    B, S, _ = x.shape
    if S not in _NC_CACHE:
        _NC_CACHE[S] = build_nc(S)
    nc = _NC_CACHE[S]
    maps = make_in_maps(x, norm_g, w_in, a_lambda_q1, a_lambda_k1, a_lambda_q2, a_lambda_k2, a_subln_g,
                        b_qnorm_g, b_knorm_g, w_proj_a, w_proj_b, w_out, final_g, S)
    res = run_bass_kernel_spmd(nc, maps, core_ids=list(range(B)))
    out = np.stack([np.asarray(r["out"], dtype=np.float32) for r in res.results], axis=0)
    return out
```

```python
import math
from contextlib import ExitStack

import numpy as np
import concourse.bass as bass
import concourse.mybir as mybir
from concourse.bass_utils import run_bass_kernel_spmd

F32 = mybir.dt.float32
BF16 = mybir.dt.bfloat16
U8 = mybir.dt.uint8
AF = mybir.ActivationFunctionType
ALU = mybir.AluOpType
AX = mybir.AxisListType

D = 2048
KC = 16
INC = 10752
OFF_AQ, OFF_AK, OFF_AV, OFF_AG = 0, 1024, 2048, 3072
OFF_BQ, OFF_BK, OFF_BV, OFF_BG, OFF_GM = 4096, 5120, 5376, 5632, 6656
EPS = 1e-6
GRID_W = 64
LAM_INIT = 0.8 - 0.6 * math.exp(0.0)
NW = 5


class Buf:
    __slots__ = ("name", "w", "wd", "r", "rd", "pr", "prd", "scratch")

    def __init__(self, name, scratch=False):
        self.name = name
        self.w = {}
        self.wd = []
        self.r = {}
        self.rd = []
        self.pr = {}
        self.prd = []
        self.scratch = scratch


class Ins:
    __slots__ = ("eng", "fn", "dma", "deps", "sig", "sem", "val", "ninc", "key")

    def __init__(self, eng, fn, dma, ninc, key):
        self.eng = eng
        self.fn = fn
        self.dma = dma
        self.deps = set()
        self.sig = False
        self.sem = None
        self.val = 0
        self.ninc = ninc
        self.key = key


class Prog:
    ENGS = ("pe", "act", "dve", "pool", "sp")

    def __init__(self):
        self.streams = {e: [] for e in self.ENGS}
        self.fence_deps = set()
        self.dmas_since_fence = []

    def op(self, eng, fn, reads=(), writes=(), pwrites=(), dma_key=None, ninc=1):
        dma = dma_key is not None
        ins = Ins(eng, fn, dma, ninc, dma_key)
        deps = ins.deps
        scratch = False
        for b in reads:
            deps.update(b.w.values())
            deps.update(b.wd)
            scratch |= b.scratch
        for b in writes:
            deps.update(b.w.values())
            deps.update(b.wd)
            deps.update(b.r.values())
            deps.update(b.rd)
            deps.update(b.pr.values())
            deps.update(b.prd)
            scratch |= b.scratch
        for b in pwrites:
            deps.update(b.r.values())
            deps.update(b.rd)
            deps.update(b.pr.values())
            deps.update(b.prd)
            scratch |= b.scratch
        if scratch:
            deps.update(self.fence_deps)
        deps.discard(ins)
        for b in reads:
            if dma:
                b.rd.append(ins)
            else:
                b.r[eng] = ins
        for b in writes:
            b.pr, b.prd = b.r, b.rd
            b.w, b.wd, b.r, b.rd = {}, [], {}, []
            if dma:
                b.wd.append(ins)
            else:
                b.w[eng] = ins
        for b in pwrites:
            if b.r or b.rd:
                b.pr, b.prd = b.r, b.rd
                b.w, b.wd, b.r, b.rd = {}, [], {}, []
            if dma:
                b.wd.append(ins)
            else:
                b.w[eng] = ins
        self.streams[eng].append(ins)
        if dma:
            self.dmas_since_fence.append(ins)
        return ins

    def fence(self):
        f = set()
        for e in self.ENGS:
            if self.streams[e]:
                f.add(self.streams[e][-1])
        f.update(self.dmas_since_fence)
        self.fence_deps = set(i for i in f)
        self.dmas_since_fence = []

    def finalize(self, nc, stack):
        for e in self.ENGS:
            for ins in self.streams[e]:
                for d in ins.deps:
                    if (not d.dma) and (not ins.dma) and d.eng == "pe" and ins.eng == "pe":
                        continue
                    d.sig = True
        self.esem = {}
        for e in ("pe", "act", "dve"):
            self.esem[e] = stack.enter_context(nc.semaphore("s_" + e))
        self.dsem = {}
        dcount = {}
        for e in self.ENGS:
            cnt = 0
            for ins in self.streams[e]:
                if ins.dma:
                    if ins.key not in self.dsem:
                        self.dsem[ins.key] = stack.enter_context(nc.semaphore("d_" + ins.key))
                        dcount[ins.key] = 0
                    dcount[ins.key] += 16 * ins.ninc
                    ins.sem = self.dsem[ins.key]
                    ins.val = dcount[ins.key]
                elif ins.sig:
                    assert e in self.esem, e
                    cnt += 1
                    ins.sem = self.esem[e]
                    ins.val = cnt

    def emit(self, eng, engobj):
        waited = {}
        for ins in self.streams[eng]:
            need = {}
            for d in ins.deps:
                if (not d.dma) and (not ins.dma) and d.eng == "pe" and eng == "pe":
                    continue
                k = id(d.sem)
                if waited.get(k, 0) >= d.val:
                    continue
                if k not in need or need[k][1] < d.val:
                    need[k] = (d.sem, d.val)
            for k, (sem, val) in need.items():
                engobj.wait_ge(sem, val)
                waited[k] = val
            r = ins.fn(engobj)
            if ins.dma:
                assert len(r) == ins.ninc, (len(r), ins.ninc)
                for x in r:
                    x.then_inc(ins.sem, 16)
            elif ins.sig:
                r.then_inc(ins.sem, 1)


def build_nc(S=2048, dbg=None, phases=("p0", "A", "B", "E")):
    assert S % 512 == 0
    NB = S // 128
    BS = 512
    TB = S // BS
    HS = S // 2
    EBS = min(512, HS)
    ETB = HS // EBS
    ROWS = S // GRID_W

    nc = bass.Bass("TRN2", target_bir_lowering=False)
    try:
        nc.allow_low_precision("bf16 matmul operands with fp32 PSUM accumulation")
    except Exception:
        pass

    def din(name, shape):
        return nc.dram_tensor(name, list(shape), F32, kind="ExternalInput").ap()

    x_d = din("x", [S, D])
    win_d = din("w_in", [D, INC])
    wpa_d = din("w_proj_a", [1024, D])
    wpb_d = din("w_proj_b", [1024, D])
    wout_d = din("w_out", [D, D])
    woutb16_d = nc.dram_tensor("w_out_bf16", [D, D], BF16, kind="Internal").ap()
    woutb16_buf = Buf("w_out_bf16")
    pcol_d = din("pcol", [128, 19])
    lamv_d = din("lamv", [128, 256])
    fg_d = din("fgb", [128, D])
    cid_d = din("cident", [128, 512])
    NDELTA = (2 * S - 128 - 512) // 128 + 1
    qaug_d = din("qaug", [128, S])
    kaug_d = din("kaug", [128, S])
    crope_d = din("crope", [128, 256])
    out_d = nc.dram_tensor("out", [S, D], F32, kind="ExternalOutput").ap()

    dbg_outs = []

    P = Prog()
    stack = ExitStack()

    def sb(name, shape, dt):
        return nc.alloc_sbuf_tensor(name, list(shape), dt).ap()

    uT = sb("uT", [128, KC * S], BF16)
    uT3 = uT.rearrange("p (k s) -> p k s", k=KC)
    oag = sb("oag", [128, 8 * S], BF16)
    oag3 = oag.rearrange("p (k s) -> p k s", k=8)
    obg = sb("obg", [128, 8 * S], BF16)
    obg3 = obg.rearrange("p (k s) -> p k s", k=8)
    wsl = sb("wsl", [128, NW * KC * 128], BF16)
    wsl4 = wsl.rearrange("p (n k c) -> p n k c", n=NW, k=KC)
    CID = sb("cid", [128, 512], BF16)
    ident = CID[:, 0:128]
    ones = CID[:, 128:256]
    swapm = CID[:, 256:384]
    diagm = CID[:, 384:512]
    PCOL = sb("pcol_s", [128, 19], F32)
    LAMV = sb("lamv_s", [128, 256], F32)
    SMALL = sb("small", [128, 64], F32)
    SSQ = sb("ssq", [128, 4 * NB], F32)
    SCRB = 54784
    SCR = sb("scr", [128, SCRB], U8)
    if S != 2048:
        WOUT = sb("wout_t", [128, KC * D], BF16)

    uTb = [Buf(f"uT{n}") for n in range(NB)]
    oagb = [Buf(f"oag{h}") for h in range(8)]
    obgb = [Buf(f"obg{h}") for h in range(8)]
    wslb = [Buf(f"wsl{i}") for i in range(NW)]
    cidb = Buf("cid")
    pcolb = Buf("pcol")
    lamvb = Buf("lamv")
    smallb = Buf("small")
    ssqb = Buf("ssq")

    PST = [nc.alloc_psum_tensor("psA", [128, 2048], F32).ap(), nc.alloc_psum_tensor("psB", [128, 2048], F32).ap()]
    PS = [PST[j // 4][:, (j % 4) * 512:(j % 4 + 1) * 512] for j in range(8)]
    PSb = [Buf(f"ps{j}") for j in range(8)]

    def psbf(j, n):
        return PS[j][:, 0:n // 2].bitcast(BF16)

    class Carver:
        def __init__(self):
            self.off = 0

        def take(self, nbytes_per_part, dt, name):
            nb_ = (nbytes_per_part + 31) // 32 * 32
            assert self.off + nb_ <= SCRB, (name, self.off, nb_, SCRB)
            ap = SCR[:, self.off:self.off + nbytes_per_part]
            self.off += nb_
            if dt is not U8:
                ap = ap.bitcast(dt)
            return ap, Buf(name, scratch=True)

    wctr = [0]
    head_carry = []

    def load_w(src_ap, kcn, label):
        i = wctr[0] % NW
        wctr[0] += 1
        dst = wsl4[:, i, 0:kcn, :]
        src = src_ap.rearrange("(k p) c -> p k c", p=128)
        P.op("pool", lambda e: [e.dma_start(out=dst, in_=src)], writes=[wslb[i]], dma_key=f"w{i}")
        return i

    gctr = [0]

    def gbank():
        j = gctr[0] % 2
        gctr[0] += 1
        return j

    def inproj_chunk(src_ap, kcn, rhs_fn, rhs_bufs_fn, ntb, bs, evac, bank_fn=gbank):
        i = load_w(src_ap, kcn, "w")
        later = None
        for tb in range(ntb):
            j = bank_fn()

            def fn(e, tb=tb, j=j, i=i):
                r = None
                for kc in range(kcn):
                    r = e.matmul(PS[j][:, 0:bs], lhsT=wsl4[:, i, kc, :], rhs=rhs_fn(kc, tb),
                                 start=(kc == 0), stop=(kc == kcn - 1))
                return r

            P.op("pe", fn, reads=[wslb[i]] + rhs_bufs_fn(tb), writes=[PSb[j]])
            while head_carry:
                head_carry.pop(0)()
            if later is not None:
                later()
            later = evac(tb, j)
        if later is not None:
            later()

    def u_rhs(kc, tb):
        return uT3[:, kc, tb * BS:(tb + 1) * BS]

    def u_bufs(tb):
        return uTb[tb * 4:(tb + 1) * 4]

    P.op("sp", lambda e: [e.dma_start(out=PCOL[:, :], in_=pcol_d[:, :]),
                          e.dma_start(out=LAMV[:, :], in_=lamv_d[:, :])],
         writes=[pcolb, lamvb], dma_key="c0", ninc=2)
    P.op("pool", lambda e: [e.dma_start(out=CID[:, :], in_=cid_d[:, :])], writes=[cidb], dma_key="c1")
    gcol = PCOL[:, 0:16]
    sub_col = PCOL[:, 16:17]
    qg_col = PCOL[:, 17:18]
    kg_col = PCOL[:, 18:19]
    prod = SMALL[:, 0:2]
    ev = SMALL[:, 2:4]
    lam = SMALL[:, 4:5]
    neglam = SMALL[:, 5:6]
    sub08 = SMALL[:, 6:7]
    LPROD = sb("lprod", [128, 128], F32)
    lprodb = Buf("lprod")
    P.op("dve", lambda e: e.tensor_tensor(out=LPROD[:, :], in0=LAMV[:, 0:128], in1=LAMV[:, 128:256], op=ALU.mult),
         reads=[lamvb], writes=[lprodb])
    P.op("dve", lambda e: e.reduce_sum(out=prod, in_=LPROD.rearrange("p (a b) -> p a b", a=2), axis=AX.X),
         reads=[lprodb], writes=[smallb])
    P.op("act", lambda e: e.activation(out=ev, in_=prod, func=AF.Exp), reads=[smallb], pwrites=[smallb])
    P.op("dve", lambda e: e.tensor_tensor(out=lam, in0=SMALL[:, 2:3], in1=SMALL[:, 3:4], op=ALU.subtract),
         reads=[smallb], pwrites=[smallb])
    P.op("dve", lambda e: e.tensor_scalar(out=neglam, in0=lam, scalar1=LAM_INIT, scalar2=-1.0, op0=ALU.add, op1=ALU.mult),
         reads=[smallb], pwrites=[smallb])
    P.op("dve", lambda e: e.tensor_scalar(out=sub08, in0=sub_col, scalar1=1.0 - LAM_INIT, scalar2=None, op0=ALU.mult),
         reads=[pcolb], pwrites=[smallb])

    def phase_0():
        cv = Carver()
        xs = [cv.take(D * 4, F32, f"xs{i}") for i in range(4)]
        xb = [cv.take(D * 2, BF16, f"xb{i}") for i in range(2)]
        junk = cv.take(D * 2, BF16, "junk")
        for nb in range(NB):
            s = nb % 2
            xs_ap, xs_b = xs[nb % 4]
            xb_ap, xb_b = xb[s]
            P.op("sp", lambda e, nb=nb, xs_ap=xs_ap: [e.dma_start(out=xs_ap, in_=x_d[nb * 128:(nb + 1) * 128, :])],
                 writes=[xs_b], dma_key=f"x{nb % 4}")
            P.op("act", lambda e, nb=nb, xs_ap=xs_ap: e.activation(out=junk[0], in_=xs_ap, func=AF.Square,
                                                                     accum_out=SSQ[:, nb:nb + 1]),
                 reads=[xs_b], writes=[junk[1]], pwrites=[ssqb])
            P.op("act", lambda e, nb=nb: e.activation(out=SSQ[:, NB + nb:NB + nb + 1], in_=SSQ[:, nb:nb + 1], func=AF.Ln,
                                                       scale=1.0 / D, bias=EPS),
                 reads=[ssqb], pwrites=[ssqb])
            P.op("act", lambda e, nb=nb: e.activation(out=SSQ[:, 2 * NB + nb:2 * NB + nb + 1],
                                                       in_=SSQ[:, NB + nb:NB + nb + 1], func=AF.Exp, scale=-0.5),
                 reads=[ssqb], pwrites=[ssqb])
            P.op("act", lambda e, nb=nb, xs_ap=xs_ap, xb_ap=xb_ap: e.activation(
                out=xb_ap, in_=xs_ap, func=AF.Copy, scale=SSQ[:, 2 * NB + nb:2 * NB + nb + 1]),
                 reads=[xs_b, ssqb], writes=[xb_b])
            for g4 in range(4):
                j = gbank()

                def fn(e, g4=g4, j=j, xb_ap=xb_ap):
                    r = None
                    for i in range(4):
                        kc = g4 * 4 + i
                        r = e.transpose(psbf(j, 512)[:, i * 128:(i + 1) * 128], xb_ap[:, kc * 128:(kc + 1) * 128], ident)
                    return r

                P.op("pe", fn, reads=[xb_b, cidb], writes=[PSb[j]])
                P.op("dve", lambda e, g4=g4, j=j, nb=nb: e.tensor_tensor(
                    out=uT3[:, g4 * 4:(g4 + 1) * 4, nb * 128:(nb + 1) * 128],
                    in0=psbf(j, 512).rearrange("p (a b) -> p a b", a=4),
                    in1=gcol[:, g4 * 4:(g4 + 1) * 4].unsqueeze(2).to_broadcast([128, 4, 128]), op=ALU.mult),
                     reads=[PSb[j], pcolb], pwrites=[uTb[nb]])
        P.fence()

    def evac_copy_act(dst_ap, dst_buf, bs):
        def ev_(tb, j):
            P.op("act", lambda e: e.activation(out=dst_ap[:, tb * bs:(tb + 1) * bs], in_=PS[j][:, 0:bs], func=AF.Copy),
                 reads=[PSb[j]], pwrites=[dst_buf])
        return ev_

    def evac_copy_dve(dst_ap, dst_buf, bs):
        def ev_(tb, j):
            P.op("dve", lambda e: e.tensor_copy(out=dst_ap[:, tb * bs:(tb + 1) * bs], in_=PS[j][:, 0:bs]),
                 reads=[PSb[j]], pwrites=[dst_buf])
        return ev_

    def evac_silu(dst_ap, dst_buf, etmp):
        def ev_(tb, j):
            t_ap, t_b = etmp[tb % 2]
            P.op("act", lambda e: e.activation(out=t_ap, in_=PS[j][:, :], func=AF.Exp, scale=-1.0),
                 reads=[PSb[j]], writes=[t_b])
            P.op("act", lambda e: e.activation(out=t_ap, in_=t_ap, func=AF.Ln, bias=1.0), reads=[t_b], writes=[t_b])
            P.op("act", lambda e: e.activation(out=t_ap, in_=t_ap, func=AF.Exp, scale=-1.0), reads=[t_b], writes=[t_b])
            P.op("dve", lambda e: e.tensor_tensor(out=dst_ap[:, tb * BS:(tb + 1) * BS], in0=PS[j][:, :], in1=t_ap, op=ALU.mult),
                 reads=[PSb[j], t_b], pwrites=[dst_buf])
        return ev_

    def v_transposes(vT_ap, vT_b, v_ap, v_b):
        for tb in range(TB):
            j = gbank()

            def fn(e, tb=tb, j=j):
                r = None
                for i in range(4):
                    t0 = tb * BS + i * 128
                    r = e.transpose(psbf(j, 512)[:, i * 128:(i + 1) * 128], vT_ap[:, t0:t0 + 128], ident)
                return r

            P.op("pe", fn, reads=[vT_b, cidb], writes=[PSb[j]])
            P.op("dve", lambda e, tb=tb, j=j: e.tensor_copy(out=v_ap[:, tb * BS:(tb + 1) * BS], in_=psbf(j, 512)),
                 reads=[PSb[j]], pwrites=[v_b])

    SC = [2, 3]
    SC4 = [2, 3, 5, 7]
    PVB = [4, 5]
    SMB = [6, 7]

    def phase_A():
        cv = Carver()
        QX = [cv.take(S * 2, BF16, f"QX{m}") for m in range(2)]
        KX = [cv.take(S * 2, BF16, f"KX{m}") for m in range(2)]
        vv, vvb = cv.take(S * 2, BF16, "v")
        sg, sgb = cv.take(S * 2, BF16, "sg")
        PP = [cv.take(2 * BS * 2, BF16, f"PP{i}") for i in range(2)]
        pairctr = [0]

        def next_pair():
            p = pairctr[0] % 2
            pairctr[0] += 1
            return (2, 3) if p == 0 else (0, 1)

        def pair_ap(b0):
            return PST[0][:, b0 * 512:(b0 + 2) * 512]
        off_T = cv.off
        T = [cv.take(BS * 4, F32, f"T{i}") for i in range(8)]
        sq = cv.take(BS * 2, BF16, "sq")
        r1, r2, t1, t2, rs = T[0], T[1], T[2], T[3], T[4]
        etmp = [T[6], T[7]]
        SQ = SCR[:, off_T:off_T + S * 2].bitcast(BF16)
        SK = SCR[:, off_T + 4096:off_T + 4096 + S * 2].bitcast(BF16)
        vT = SCR[:, off_T + 8192:off_T + 8192 + S * 2].bitcast(BF16)
        sqbufs = [T[0][1], T[1][1]]
        skbufs = [T[2][1], T[3][1]]
        vTbufs = [T[4][1], T[5][1]]
        P.op("pool", lambda e: [e.dma_start(out=QX[0][0], in_=qaug_d[:, :]), e.dma_start(out=QX[1][0], in_=qaug_d[:, :]),
                                e.dma_start(out=KX[0][0], in_=kaug_d[:, :]), e.dma_start(out=KX[1][0], in_=kaug_d[:, :])],
             writes=[QX[0][1], QX[1][1], KX[0][1], KX[1][1]], dma_key="c5", ninc=4)

        def evac_stage(dst_ap, bufs, eng, scale):
            def ev_(tb, j):
                kw = dict(writes=bufs) if tb == 0 else dict(pwrites=bufs)
                if eng == "act":
                    P.op("act", lambda e: e.activation(out=dst_ap[:, tb * BS:(tb + 1) * BS], in_=PS[j][:, :], func=AF.Copy,
                                                       scale=float(scale)), reads=[PSb[j]], **kw)
                else:
                    P.op("dve", lambda e: e.tensor_copy(out=dst_ap[:, tb * BS:(tb + 1) * BS], in_=PS[j][:, :]),
                         reads=[PSb[j]], **kw)
            return ev_

        def v_transposes_a():
            for tb in range(TB):
                j = gbank()

                def fn(e, tb=tb, j=j):
                    r = None
                    for i in range(4):
                        t0 = tb * BS + i * 128
                        r = e.transpose(psbf(j, 512)[:, i * 128:(i + 1) * 128], vT[:, t0:t0 + 128], ident)
                    return r

                P.op("pe", fn, reads=vTbufs + [cidb], writes=[PSb[j]])
                P.op("dve", lambda e, tb=tb, j=j: e.tensor_copy(out=vv[:, tb * BS:(tb + 1) * BS], in_=psbf(j, 512)),
                     reads=[PSb[j]], pwrites=[vvb])

        for h in range(8):
            slope = 2.0 ** (-(h + 1))
            qscale = 2.0 ** (h - 2)
            inproj_chunk(win_d[:, OFF_AQ + h * 128:OFF_AQ + (h + 1) * 128], KC, u_rhs, u_bufs, TB, BS,
                         evac_stage(SQ, sqbufs, "act", qscale))
            inproj_chunk(win_d[:, OFF_AK + h * 128:OFF_AK + (h + 1) * 128], KC, u_rhs, u_bufs, TB, BS,
                         evac_stage(SK, skbufs, "dve", 1.0))
            P.op("sp", lambda e: [e.dma_start(out=QX[0][0][0:64, :], in_=SQ[0:64, :]),
                                  e.dma_start(out=QX[1][0][0:64, :], in_=SQ[64:128, :])],
                 reads=sqbufs, writes=[QX[0][1], QX[1][1]], dma_key="qx", ninc=2)
            P.op("sp", lambda e: [e.dma_start(out=KX[0][0][0:64, :], in_=SK[0:64, :]),
                                  e.dma_start(out=KX[1][0][0:64, :], in_=SK[64:128, :])],
                 reads=skbufs, writes=[KX[0][1], KX[1][1]], dma_key="kx", ninc=2)
            inproj_chunk(win_d[:, OFF_AV + h * 128:OFF_AV + (h + 1) * 128], KC, u_rhs, u_bufs, TB, BS,
                         evac_stage(vT, vTbufs, "act", 1.0))
            v_transposes_a()
            inproj_chunk(win_d[:, OFF_AG + h * 128:OFF_AG + (h + 1) * 128], KC, u_rhs, u_bufs, TB, BS,
                         evac_silu(sg, sgb, etmp))

            tiles = [(qb, kb) for qb in range(TB) for kb in range(NB)]

            def scores(i, qb, kb, h=h, slope=slope):
                delta = qb * BS - kb * 128
                if delta >= 128:
                    regions = [(0, BS, 128, False)]
                elif delta <= -BS:
                    regions = [(0, BS, 96, False)]
                else:
                    sidx = (-delta) // 128
                    regions = []
                    if sidx > 0:
                        regions.append((0, sidx * 128, 96, False))
                    regions.append((sidx * 128, (sidx + 1) * 128, 64, True))
                    if sidx < 3:
                        regions.append(((sidx + 1) * 128, BS, 128, False))
                b0, b1 = next_pair()
                for m, j in ((0, b0), (1, b1)):
                    def fn(e, m=m, j=j):
                        r = None
                        for (c0, c1, kw_, dg) in regions:
                            r = e.matmul(PS[j][:, c0:c1], lhsT=KX[m][0][0:kw_, kb * 128:(kb + 1) * 128],
                                         rhs=QX[m][0][0:kw_, qb * BS + c0:qb * BS + c1], start=True, stop=not dg)
                            if dg:
                                r = e.matmul(PS[j][:, c0:c1], lhsT=ident, rhs=diagm, start=False, stop=True)
                        return r
                    P.op("pe", fn, reads=[KX[m][1], QX[m][1], cidb], writes=[PSb[j]])
                pp_ap, pp_b = PP[i % 2]
                P.op("act", lambda e: e.activation(out=pp_ap, in_=pair_ap(b0), func=AF.Exp, scale=float(slope)),
                     reads=[PSb[b0], PSb[b1]], writes=[pp_b])

            def pvsm(i, qb, kb):
                pp_ap, pp_b = PP[i % 2]
                for m in range(2):
                    p_ap = pp_ap[:, m * BS:(m + 1) * BS]
                    st, sp_ = (kb == 0), (kb == NB - 1)
                    kw = dict(writes=[PSb[PVB[m]]]) if st else dict(pwrites=[PSb[PVB[m]]])
                    P.op("pe", lambda e, m=m, p_ap=p_ap, st=st, sp_=sp_: e.matmul(
                        PS[PVB[m]][:, :], lhsT=vv[:, kb * 128:(kb + 1) * 128], rhs=p_ap, start=st, stop=sp_),
                         reads=[vvb, pp_b], **kw)
                    kw = dict(writes=[PSb[SMB[m]]]) if st else dict(pwrites=[PSb[SMB[m]]])
                    P.op("pe", lambda e, m=m, p_ap=p_ap, st=st, sp_=sp_: e.matmul(
                        PS[SMB[m]][:, :], lhsT=ones, rhs=p_ap, start=st, stop=sp_),
                         reads=[cidb, pp_b], **kw)

            def epi1(qb):
                for rr, m_ in ((r1, 0), (r2, 1)):
                    P.op("act", lambda e, rr=rr, m_=m_: e.activation(out=rr[0], in_=PS[SMB[m_]][:, :], func=AF.Ln),
                         reads=[PSb[SMB[m_]]], writes=[rr[1]])
                    P.op("act", lambda e, rr=rr: e.activation(out=rr[0], in_=rr[0], func=AF.Exp, scale=-1.0),
                         reads=[rr[1]], writes=[rr[1]])
                P.op("dve", lambda e: e.tensor_copy(out=t1[0], in_=PS[PVB[0]][:, :]), reads=[PSb[PVB[0]]], writes=[t1[1]])
                P.op("dve", lambda e: e.tensor_copy(out=t2[0], in_=PS[PVB[1]][:, :]), reads=[PSb[PVB[1]]], writes=[t2[1]])
                P.op("dve", lambda e: e.tensor_tensor(out=t1[0], in0=t1[0], in1=r1[0], op=ALU.mult),
                     reads=[t1[1], r1[1]], writes=[t1[1]])
                P.op("dve", lambda e: e.tensor_tensor(out=t2[0], in0=t2[0], in1=r2[0], op=ALU.mult),
                     reads=[t2[1], r2[1]], writes=[t2[1]])
                P.op("dve", lambda e: e.scalar_tensor_tensor(out=r1[0], in0=t2[0], scalar=neglam, in1=t1[0],
                                                              op0=ALU.mult, op1=ALU.add),
                     reads=[t1[1], t2[1], smallb], writes=[r1[1]])
                P.op("dve", lambda e: e.tensor_tensor(out=sq[0], in0=r1[0], in1=r1[0], op=ALU.mult),
                     reads=[r1[1]], writes=[sq[1]])

            def epi2(qb, h=h, bank=None):
                j = next_pair()[0] if bank is None else bank
                P.op("pe", lambda e, j=j: e.matmul(PS[j][:, :], lhsT=ones, rhs=sq[0], start=True, stop=True),
                     reads=[cidb, sq[1]], writes=[PSb[j]])
                P.op("act", lambda e, j=j: e.activation(out=rs[0], in_=PS[j][:, :], func=AF.Ln, scale=1.0 / 128, bias=EPS),
                     reads=[PSb[j]], writes=[rs[1]])
                P.op("act", lambda e: e.activation(out=rs[0], in_=rs[0], func=AF.Exp, scale=-0.5),
                     reads=[rs[1]], writes=[rs[1]])
                P.op("dve", lambda e: e.tensor_tensor(out=t1[0], in0=r1[0], in1=rs[0], op=ALU.mult),
                     reads=[r1[1], rs[1]], writes=[t1[1]])
                P.op("dve", lambda e: e.scalar_tensor_tensor(
                    out=oag3[:, h, qb * BS:(qb + 1) * BS], in0=t1[0], scalar=sub08, in1=sg[:, qb * BS:(qb + 1) * BS],
                    op0=ALU.mult, op1=ALU.mult),
                     reads=[t1[1], smallb, sgb], pwrites=[oagb[h]])

            pending = None
            for i, (qb, kb) in enumerate(tiles):
                scores(i, qb, kb)
                if i > 0:
                    pq, pk = tiles[i - 1]
                    pvsm(i - 1, pq, pk)
                    if pk == NB - 1:
                        epi1(pq)
                        pending = (pq, i + 4)
                if pending is not None and i >= pending[1]:
                    epi2(pending[0])
                    pending = None
            pq, pk = tiles[-1]
            pvsm(len(tiles) - 1, pq, pk)
            if pending is not None:
                epi2(pending[0])
                pending = None
            epi1(pq)
            if h < 7:
                head_carry.append(lambda pq=pq, epi2=epi2: epi2(pq, bank=SMB[0]))
            else:
                epi2(pq)
        if dbg is not None:
            dbg_outs.append(("dbg_oag", oag, [128, 8 * S], BF16, oagb))
        P.fence()

    def phase_B():
        cv = Carver()
        qT, qTb = cv.take(S * 2, BF16, "qT")
        kT, kTb = cv.take(S * 2, BF16, "kT")
        vT, vTb = cv.take(S * 2, BF16, "vT")
        vv, vvb = cv.take(S * 2, BF16, "v")
        sg, sgb = cv.take(S * 2, BF16, "sg")
        ROPE, ropeb = cv.take(256 * 4, F32, "rope")
        rope3 = ROPE.rearrange("p (a b) -> p a b", a=4)
        PP = [cv.take(2 * BS * 2, BF16, f"PP{i}") for i in range(2)]
        etmp = [cv.take(BS * 4, F32, f"etmp{i}") for i in range(2)]
        raw = [cv.take(BS * 4, F32, f"raw{i}") for i in range(2)]
        kp = [cv.take(BS * 2, BF16, f"kp{i}") for i in range(2)]
        sqb = [cv.take(BS * 2, BF16, f"sq{i}") for i in range(2)]
        rsb = [cv.take(BS * 4, F32, f"rs{i}") for i in range(2)]
        ra = cv.take(BS * 4, F32, "ra")
        rb = cv.take(BS * 4, F32, "rb")
        r1 = cv.take(BS * 4, F32, "r1")
        t1 = cv.take(BS * 4, F32, "t1")
        P.op("sp", lambda e: [e.dma_start(out=ROPE, in_=crope_d[:, :])], writes=[ropeb], dma_key="c3")
        nrow = BS // GRID_W

        def evac_normrope(dst_ap, dst_buf, gcol_ap):
            def ev_(tb, j):
                s = tb % 2
                P.op("act", lambda e: e.activation(out=kp[s][0], in_=PS[j][:, :], func=AF.Copy, scale=gcol_ap),
                     reads=[PSb[j], pcolb], writes=[kp[s][1]])
                P.op("act", lambda e: e.activation(out=sqb[s][0], in_=PS[j][:, :], func=AF.Square),
                     reads=[PSb[j]], writes=[sqb[s][1]])
                return lambda: part2(tb, s)

            def part2(tb, s):
                P.op("pe", lambda e: e.matmul(PS[SC[0]][:, :], lhsT=ones, rhs=sqb[s][0], start=True, stop=True),
                     reads=[cidb, sqb[s][1]], writes=[PSb[SC[0]]])
                P.op("pe", lambda e: e.matmul(PS[SC[1]][:, :], lhsT=swapm, rhs=kp[s][0], start=True, stop=True),
                     reads=[cidb, kp[s][1]], writes=[PSb[SC[1]]])
                P.op("act", lambda e: e.activation(out=rsb[s][0], in_=PS[SC[0]][:, :], func=AF.Ln, scale=1.0 / 128, bias=EPS),
                     reads=[PSb[SC[0]]], writes=[rsb[s][1]])
                P.op("act", lambda e: e.activation(out=rsb[s][0], in_=rsb[s][0], func=AF.Exp, scale=-0.5),
                     reads=[rsb[s][1]], writes=[rsb[s][1]])
                r0 = tb * nrow
                for half in range(2):
                    ps_ = slice(64 * half, 64 * half + 64)
                    if half == 0:
                        ct = rope3[0:64, 0, r0:r0 + nrow].unsqueeze(2).to_broadcast([64, nrow, 64])
                        st = rope3[0:64, 1, r0:r0 + nrow].unsqueeze(2).to_broadcast([64, nrow, 64])
                    else:
                        ct = rope3[64:128, 2, :].unsqueeze(1).to_broadcast([64, nrow, 64])
                        st = rope3[64:128, 3, :].unsqueeze(1).to_broadcast([64, nrow, 64])
                    P.op("dve", lambda e, ps_=ps_, ct=ct: e.tensor_tensor(
                        out=ra[0][ps_, :].rearrange("p (a b) -> p a b", b=64),
                        in0=kp[s][0][ps_, :].rearrange("p (a b) -> p a b", b=64), in1=ct, op=ALU.mult),
                         reads=[kp[s][1], ropeb], pwrites=[ra[1]])
                    P.op("dve", lambda e, ps_=ps_, st=st: e.tensor_tensor(
                        out=rb[0][ps_, :].rearrange("p (a b) -> p a b", b=64),
                        in0=PS[SC[1]][ps_, :].rearrange("p (a b) -> p a b", b=64), in1=st, op=ALU.mult),
                         reads=[PSb[SC[1]], ropeb], pwrites=[rb[1]])
                P.op("dve", lambda e: e.tensor_tensor(out=ra[0], in0=ra[0], in1=rb[0], op=ALU.add),
                     reads=[ra[1], rb[1]], writes=[ra[1]])
                P.op("dve", lambda e: e.tensor_tensor(out=dst_ap[:, tb * BS:(tb + 1) * BS], in0=ra[0], in1=rsb[s][0], op=ALU.mult),
                     reads=[ra[1], rsb[s][1]], pwrites=[dst_buf])
            return ev_

        scale_b = 128.0 ** -0.5
        for gk in range(2):
            inproj_chunk(win_d[:, OFF_BK + gk * 128:OFF_BK + (gk + 1) * 128], KC, u_rhs, u_bufs, TB, BS,
                         evac_normrope(kT, kTb, kg_col))
            inproj_chunk(win_d[:, OFF_BV + gk * 128:OFF_BV + (gk + 1) * 128], KC, u_rhs, u_bufs, TB, BS,
                         evac_copy_act(vT, vTb, BS))
            v_transposes(vT, vTb, vv, vvb)
            for hq in range(4):
                h = gk * 4 + hq
                inproj_chunk(win_d[:, OFF_BQ + h * 128:OFF_BQ + (h + 1) * 128], KC, u_rhs, u_bufs, TB, BS,
                             evac_normrope(qT, qTb, qg_col))
                inproj_chunk(win_d[:, OFF_BG + h * 128:OFF_BG + (h + 1) * 128], KC, u_rhs, u_bufs, TB, BS,
                             evac_silu(sg, sgb, etmp))
                tiles = [(qb, kb) for qb in range(TB) for kb in range(NB)]

                def scores_pair(p):
                    b0, b1 = (2, 3) if p % 2 == 0 else (0, 1)
                    for t_, j in ((2 * p, b0), (2 * p + 1, b1)):
                        qb, kb = tiles[t_]
                        P.op("pe", lambda e, j=j, qb=qb, kb=kb: e.matmul(
                            PS[j][:, :], lhsT=kT[:, kb * 128:(kb + 1) * 128], rhs=qT[:, qb * BS:(qb + 1) * BS],
                            start=True, stop=True), reads=[kTb, qTb], writes=[PSb[j]])
                    pp_ap, pp_b = PP[p % 2]
                    P.op("act", lambda e: e.activation(out=pp_ap, in_=PST[0][:, b0 * 512:(b0 + 2) * 512], func=AF.Exp,
                                                       scale=scale_b), reads=[PSb[b0], PSb[b1]], writes=[pp_b])

                def pvsm(t_):
                    qb, kb = tiles[t_]
                    pp_ap, pp_b = PP[(t_ // 2) % 2]
                    p_ap = pp_ap[:, (t_ % 2) * BS:(t_ % 2 + 1) * BS]
                    pvb, smb = PVB[qb % 2], SMB[qb % 2]
                    st, sp_ = (kb == 0), (kb == NB - 1)
                    kw = dict(writes=[PSb[pvb]]) if st else dict(pwrites=[PSb[pvb]])
                    P.op("pe", lambda e: e.matmul(PS[pvb][:, :], lhsT=vv[:, kb * 128:(kb + 1) * 128], rhs=p_ap,
                                                  start=st, stop=sp_), reads=[vvb, pp_b], **kw)
                    kw = dict(writes=[PSb[smb]]) if st else dict(pwrites=[PSb[smb]])
                    P.op("pe", lambda e: e.matmul(PS[smb][:, :], lhsT=ones, rhs=p_ap, start=st, stop=sp_),
                         reads=[cidb, pp_b], **kw)

                def epilogue(qb, h=h):
                    pvb, smb = PVB[qb % 2], SMB[qb % 2]
                    P.op("act", lambda e: e.activation(out=r1[0], in_=PS[smb][:, :], func=AF.Ln),
                         reads=[PSb[smb]], writes=[r1[1]])
                    P.op("act", lambda e: e.activation(out=r1[0], in_=r1[0], func=AF.Exp, scale=-1.0),
                         reads=[r1[1]], writes=[r1[1]])
                    P.op("dve", lambda e: e.tensor_tensor(out=t1[0], in0=PS[pvb][:, :], in1=r1[0], op=ALU.mult),
                         reads=[PSb[pvb], r1[1]], writes=[t1[1]])
                    P.op("dve", lambda e: e.tensor_tensor(out=obg3[:, h, qb * BS:(qb + 1) * BS], in0=t1[0],
                                                          in1=sg[:, qb * BS:(qb + 1) * BS], op=ALU.mult),
                         reads=[t1[1], sgb], pwrites=[obgb[h]])

                npair = len(tiles) // 2
                for p in range(npair + 1):
                    if p < npair:
                        scores_pair(p)
                    if p >= 1:
                        for t_ in (2 * p - 2, 2 * p - 1):
                            pvsm(t_)
                            if tiles[t_][1] == NB - 1:
                                epilogue(tiles[t_][0])
        if dbg is not None:
            dbg_outs.append(("dbg_obg", obg, [128, 8 * S], BF16, obgb))
        P.fence()

    def phase_E():
        cv = Carver()
        P.op("pool", lambda e: [e.dma_start(out=woutb16_d[r * 512:(r + 1) * 512, :], in_=wout_d[r * 512:(r + 1) * 512, :])
                                for r in range(4)], writes=[woutb16_buf], dma_key="wcast", ninc=4)
        mT0, mT0b = cv.take(KC * HS * 2, BF16, "mT0")
        mT0_3 = mT0.rearrange("p (k s) -> p k s", k=KC)
        off_after_m = cv.off
        sa = [cv.take(EBS * 4, F32, f"sa{i}") for i in range(2)]
        sbt = [cv.take(EBS * 4, F32, f"sb{i}") for i in range(2)]
        tt = [cv.take(EBS * 4, F32, f"tt{i}") for i in range(2)]
        ectr = [0]

        for H in range(2):
            tok0 = H * HS
            for c in range(16):
                iw = [load_w(win_d[:, OFF_GM + c * 128:OFF_GM + (c + 1) * 128], KC, "gma"),
                      load_w(win_d[:, OFF_GM + D + c * 128:OFF_GM + D + (c + 1) * 128], KC, "gmb"),
                      load_w(wpa_d[:, c * 128:(c + 1) * 128], 8, "wpa"),
                      load_w(wpb_d[:, c * 128:(c + 1) * 128], 8, "wpb")]
                for tb in range(ETB):
                    t0 = tok0 + tb * EBS
                    st_ = ectr[0] % 2
                    ectr[0] += 1
                    banks = [4 * st_ + q for q in range(4)]
                    ub = uTb[t0 // 128:(t0 + EBS) // 128]

                    def mk(slot, kcn, rhs3, rbufs, bank, t0):
                        def fn(e):
                            r = None
                            for kc in range(kcn):
                                r = e.matmul(PS[bank][:, 0:EBS], lhsT=wsl4[:, slot, kc, :], rhs=rhs3[:, kc, t0:t0 + EBS],
                                             start=(kc == 0), stop=(kc == kcn - 1))
                            return r
                        P.op("pe", fn, reads=[wslb[slot]] + list(rbufs), writes=[PSb[bank]])

                    mk(iw[0], KC, uT3, ub, banks[0], t0)
                    mk(iw[1], KC, uT3, ub, banks[1], t0)
                    mk(iw[2], 8, oag3, oagb, banks[2], t0)
                    mk(iw[3], 8, obg3, obgb, banks[3], t0)
                    P.op("act", lambda e, st_=st_, b=banks[0]: e.activation(out=sa[st_][0], in_=PS[b][:, 0:EBS], func=AF.Sigmoid),
                         reads=[PSb[banks[0]]], writes=[sa[st_][1]])
                    P.op("act", lambda e, st_=st_, b=banks[1]: e.activation(out=sbt[st_][0], in_=PS[b][:, 0:EBS], func=AF.Sigmoid),
                         reads=[PSb[banks[1]]], writes=[sbt[st_][1]])
                    P.op("dve", lambda e, st_=st_, b=banks[2]: e.tensor_tensor(out=sa[st_][0], in0=PS[b][:, 0:EBS], in1=sa[st_][0], op=ALU.mult),
                         reads=[PSb[banks[2]], sa[st_][1]], writes=[sa[st_][1]])
                    P.op("dve", lambda e, st_=st_, b=banks[3]: e.tensor_tensor(out=sbt[st_][0], in0=PS[b][:, 0:EBS], in1=sbt[st_][0], op=ALU.mult),
                         reads=[PSb[banks[3]], sbt[st_][1]], writes=[sbt[st_][1]])
                    if H == 0:
                        dst = mT0_3[:, c, tb * EBS:(tb + 1) * EBS]
                        kw = dict(pwrites=[mT0b])
                    else:
                        dst = uT3[:, c, tb * EBS:(tb + 1) * EBS]
                        kw = dict(pwrites=uTb[(tb * EBS) // 128:((tb + 1) * EBS) // 128])
                    P.op("dve", lambda e, st_=st_, dst=dst: e.tensor_tensor(out=dst, in0=sa[st_][0], in1=sbt[st_][0], op=ALU.add),
                         reads=[sa[st_][1], sbt[st_][1]], **kw)

        if S == 2048:
            def wo(kc):
                return (oag3 if kc < 8 else obg3)[:, kc % 8, :]
            wo_bufs = oagb + obgb
        else:
            WOUT3 = WOUT.rearrange("p (k c) -> p k c", k=KC)
            woutb = Buf("wout")

            def wo(kc):
                return WOUT3[:, kc, :]
            wo_bufs = [woutb]
        P.op("sp", lambda e: [e.dma_start(out=wo(kc), in_=woutb16_d[kc * 128:(kc + 1) * 128, :]) for kc in range(KC)],
             reads=[woutb16_buf], writes=wo_bufs, dma_key="wout", ninc=KC)
        FG = wsl[:, 0:2 * D].bitcast(F32)
        JK = wsl[:, 2 * D:3 * D]
        P.op("sp", lambda e: [e.dma_start(out=FG, in_=fg_d[:, :])], writes=[wslb[0], wslb[1]], dma_key="c4")
        P.fence()
        cv.off = off_after_m
        xh = [cv.take(D * 4, F32, f"xh{i}") for i in range(2)]
        uhalf = uTb[0:HS // 128]
        for ts in range(NB):
            st_ = ts % 2
            tloc = (ts * 128) % HS
            if ts * 128 < HS:
                mt3, mbufs = mT0_3, [mT0b]
            else:
                mt3, mbufs = uT3, uhalf
            for ob in range(4):
                bank = 4 * st_ + ob

                def fn(e, bank=bank, ob=ob, mt3=mt3, tloc=tloc):
                    r = None
                    for kc in range(KC):
                        r = e.matmul(PS[bank][:, :], lhsT=mt3[:, kc, tloc:tloc + 128], rhs=wo(kc)[:, ob * 512:(ob + 1) * 512],
                                     start=(kc == 0), stop=(kc == KC - 1))
                    return r
                P.op("pe", fn, reads=mbufs + wo_bufs, writes=[PSb[bank]])
            xh_ap, xh_b = xh[st_]
            P.op("sp", lambda e, ts=ts, xh_ap=xh_ap: [e.dma_start(out=xh_ap, in_=x_d[ts * 128:(ts + 1) * 128, :])],
                 writes=[xh_b], dma_key=f"x{st_}")
            for q in range(4):
                P.op("dve", lambda e, st_=st_, xh_ap=xh_ap, q=q: e.tensor_tensor(
                    out=xh_ap[:, q * 512:(q + 1) * 512], in0=PS[4 * st_ + q][:, :], in1=xh_ap[:, q * 512:(q + 1) * 512], op=ALU.add),
                     reads=[PSb[4 * st_ + q], xh_b], writes=[xh_b])
            P.op("act", lambda e, ts=ts, xh_ap=xh_ap: e.activation(out=JK, in_=xh_ap, func=AF.Square,
                                                                     accum_out=SSQ[:, 3 * NB + ts:3 * NB + ts + 1]),
                 reads=[xh_b], writes=[wslb[2]], pwrites=[ssqb])
            P.op("act", lambda e, ts=ts: e.activation(out=SSQ[:, NB + ts:NB + ts + 1], in_=SSQ[:, 3 * NB + ts:3 * NB + ts + 1],
                                                       func=AF.Ln, scale=1.0 / D, bias=EPS),
                 reads=[ssqb], pwrites=[ssqb])
            P.op("act", lambda e, ts=ts: e.activation(out=SSQ[:, 2 * NB + ts:2 * NB + ts + 1], in_=SSQ[:, NB + ts:NB + ts + 1],
                                                       func=AF.Exp, scale=-0.5),
                 reads=[ssqb], pwrites=[ssqb])
            P.op("dve", lambda e, ts=ts, xh_ap=xh_ap: e.scalar_tensor_tensor(
                out=xh_ap, in0=xh_ap, scalar=SSQ[:, 2 * NB + ts:2 * NB + ts + 1], in1=FG, op0=ALU.mult, op1=ALU.mult),
                 reads=[xh_b, ssqb, wslb[0], wslb[1]], writes=[xh_b])
            P.op("sp", lambda e, ts=ts, xh_ap=xh_ap: [e.dma_start(out=out_d[ts * 128:(ts + 1) * 128, :], in_=xh_ap)],
                 reads=[xh_b], dma_key=f"o{st_}")
        last = [i for i in P.streams["sp"] if i.dma and i.key in ("o0", "o1")][-2:]
        fin = P.op("sp", lambda e: e.nop(), reads=[xh[0][1], xh[1][1]], writes=[xh[0][1], xh[1][1]])

    if "p0" in phases:
        phase_0()
    if "A" in phases:
        phase_A()
    if "B" in phases:
        phase_B()
    if "E" in phases:
        phase_E()

    dbg_aps = {}
    for (name, ap, shape, dt, bufs) in dbg_outs:
        d = nc.dram_tensor(name, list(shape), dt, kind="ExternalOutput").ap()
        dbg_aps[name] = d
        P.op("sp", lambda e, d=d, ap=ap: [e.dma_start(out=d[:, :], in_=ap)], reads=list(bufs), dma_key="dbg_" + name)
    if dbg_outs:
        dl = [i for i in P.streams["sp"] if i.dma and i.key.startswith("dbg_")]
        fb = Buf("fin")
        for i in dl:
            fb.wd.append(i)
        P.op("sp", lambda e: e.nop(), reads=[fb])
    if "E" not in phases:
        pass

    P.finalize(nc, stack)
    with nc.Block() as block:
        @block.tensor
        def _(e):
            P.emit("pe", e)

        @block.scalar
        def _(e):
            P.emit("act", e)

        @block.vector
        def _(e):
            P.emit("dve", e)

        @block.gpsimd
        def _(e):
            P.emit("pool", e)

        @block.sync
        def _(e):
            P.emit("sp", e)
    stack.close()
    return nc


def make_consts(S):
    p = np.arange(128, dtype=np.float64)[:, None]
    ident = np.eye(128, dtype=np.float32)
    ones = np.ones((128, 128), np.float32)
    swap = np.zeros((128, 128), np.float32)
    for i in range(128):
        sec, r = divmod(i, 64)
        partner = sec * 64 + (r + 32) % 64
        swap[i, partner] = 1.0
    pj = np.arange(128, dtype=np.float32)
    diag = -np.abs(pj[None, :] - pj[:, None])
    cident = np.concatenate([ident, ones, swap, diag], axis=1).astype(np.float32)
    j = np.arange(512, dtype=np.float64)[None, :]
    B0 = (j - p)
    c = np.arange(896, dtype=np.float64)[None, :]
    Bm = np.abs(c - p - 384.0)
    Bhi = 256.0 * np.floor(Bm / 256.0)
    cbias = np.concatenate([-Bhi, -(Bm - Bhi)], axis=1).astype(np.float32)
    n = 32
    inv = (10000.0 ** (-np.arange(0, 64, 2, dtype=np.float32) / 64.0)).astype(np.float32)
    d = np.arange(64)
    invd = inv[d % 32]
    sign = np.where(d < 32, -1.0, 1.0).astype(np.float32)
    rows = np.arange(32, dtype=np.float32)
    cols = np.arange(64, dtype=np.float32)
    ang_r = (rows[None, :] * invd[:, None]).astype(np.float32)
    ang_c = (cols[None, :] * invd[:, None]).astype(np.float32)
    crope = np.zeros((128, 4, 64), np.float32)
    crope[0:64, 0, 0:32] = np.cos(ang_r)
    crope[0:64, 1, 0:32] = np.sin(ang_r) * sign[:, None]
    crope[64:128, 2, :] = np.cos(ang_c)
    crope[64:128, 3, :] = np.sin(ang_c) * sign[:, None]
    nd = (2 * S - 128 - 512) // 128 + 1
    abt = np.zeros((128, 8 * nd), np.float32)
    for h in range(8):
        slope = 2.0 ** (-(h + 1))
        for di in range(nd):
            delta = di * 128 - (S - 128)
            pp = np.arange(128, dtype=np.float64)
            if delta >= 128:
                abt[:, h * nd + di] = slope * (pp - delta)
            elif delta <= -512:
                abt[:, h * nd + di] = slope * (delta - pp)
    t = np.arange(S)
    hi = (256 * (t // 256)).astype(np.float32)
    lo = (t % 256).astype(np.float32)
    one = np.ones(S, np.float32)
    qaug = np.zeros((128, S), np.float32)
    kaug = np.zeros((128, S), np.float32)
    qaug[64], qaug[65], qaug[66], qaug[67] = hi, lo, -one, -one
    qaug[96], qaug[97], qaug[98], qaug[99] = -2 * hi, -2 * lo, 2 * one, 2 * one
    for r0 in (64, 96):
        kaug[r0], kaug[r0 + 1], kaug[r0 + 2], kaug[r0 + 3] = one, one, hi, lo
    return cident, cbias, crope.reshape(128, 256), abt, qaug, kaug


def make_in_maps(x, norm_g, w_in, a_lambda_q1, a_lambda_k1, a_lambda_q2, a_lambda_k2, a_subln_g, b_qnorm_g,
                 b_knorm_g, w_proj_a, w_proj_b, w_out, final_g, S):
    f = lambda a: np.ascontiguousarray(np.asarray(a, dtype=np.float32))
    cident, cbias, crope, abt, qaug, kaug = make_consts(S)
    pcol = np.zeros((128, 19), np.float32)
    pcol[:, 0:16] = f(norm_g).reshape(16, 128).T
    pcol[:, 16] = f(a_subln_g).reshape(128)
    pcol[:, 17] = f(b_qnorm_g).reshape(128)
    pcol[:, 18] = f(b_knorm_g).reshape(128)
    lamrow = np.concatenate([f(a_lambda_q1).reshape(64), f(a_lambda_q2).reshape(64),
                             f(a_lambda_k1).reshape(64), f(a_lambda_k2).reshape(64)])
    lamv = np.ascontiguousarray(np.broadcast_to(lamrow[None, :], (128, 256)))
    fgb = np.ascontiguousarray(np.broadcast_to(f(final_g).reshape(1, D), (128, D)))
    shared = {
        "w_in": f(w_in).reshape(D, INC), "w_proj_a": f(w_proj_a).reshape(1024, D),
        "w_proj_b": f(w_proj_b).reshape(1024, D), "w_out": f(w_out).reshape(D, D),
        "pcol": pcol, "lamv": lamv, "fgb": fgb, "cident": cident, "crope": crope, "qaug": qaug, "kaug": kaug,
    }
    xx = f(x)
    maps = []
    for b in range(xx.shape[0]):
        m = dict(shared)
        m["x"] = np.ascontiguousarray(xx[b])
        maps.append(m)
    return maps


_NC_CACHE = {}


def kernel(x, norm_g, w_in, a_lambda_q1, a_lambda_k1, a_lambda_q2, a_lambda_k2, a_subln_g, b_qnorm_g,
           b_knorm_g, w_proj_a, w_proj_b, w_out, final_g):
    x = np.asarray(x)
    B, S, _ = x.shape
    if S not in _NC_CACHE:
        _NC_CACHE[S] = build_nc(S)
    nc = _NC_CACHE[S]
    maps = make_in_maps(x, norm_g, w_in, a_lambda_q1, a_lambda_k1, a_lambda_q2, a_lambda_k2, a_subln_g,
                        b_qnorm_g, b_knorm_g, w_proj_a, w_proj_b, w_out, final_g, S)
    res = run_bass_kernel_spmd(nc, maps, core_ids=list(range(B)))
    out = np.stack([np.asarray(r["out"], dtype=np.float32) for r in res.results], axis=0)
    return out
```
